# Optimizing a Trainium2 kernel written in Bass

```python
import math
import jax, jax.numpy as jnp
from jax import lax
import numpy as np

D_MODEL = 1024
BATCH = 2
SEQ = 8192
DEPTH = 4

GRID_W = 64
HEAD_DIM = 64
BLOCK = 128
NA_HEADS = 8
NA_WIN_ROWS = 8
NA_WIN_COLS = 16
DIFF_HEADS = 4
SWA_Q_HEADS = 16
SWA_KV_HEADS = 4
SWA_WINDOW = 128
PEER_HEADS = 8
PEER_N_KEYS = 128
PEER_N_EXPERTS = PEER_N_KEYS * PEER_N_KEYS
PEER_KEY_DIM = 256
PEER_TOPK = 16

ROPE_THETA = 10000.0
LN_EPS = 1e-5
NEG = -1e30
DEEPNORM_ALPHA = (2 * DEPTH) ** 0.25
DEEPNORM_BETA = (8 * DEPTH) ** -0.25
N_EVEN = (DEPTH + 1) // 2
N_ODD = DEPTH // 2

A_WIDTH = NA_HEADS * HEAD_DIM
B_QK_WIDTH = DIFF_HEADS * 2 * HEAD_DIM
B_V_WIDTH = DIFF_HEADS * 2 * HEAD_DIM
EVEN_IN = 3 * A_WIDTH + 2 * B_QK_WIDTH + B_V_WIDTH
C_Q = SWA_Q_HEADS * HEAD_DIM
C_KV = SWA_KV_HEADS * HEAD_DIM
ODD_IN = C_Q + 2 * C_KV

kernel_name = "hybrid_natten_diff_swa_peer_encoder"


def layer_norm(x, g, b):
    xf = x.astype(jnp.float32)
    mu = jnp.mean(xf, -1, keepdims=True)
    var = jnp.mean(jnp.square(xf - mu), -1, keepdims=True)
    return ((xf - mu) * lax.rsqrt(var + LN_EPS) * g.astype(jnp.float32) + b.astype(jnp.float32)).astype(x.dtype)


def rope_tables(seq):
    inv = 1.0 / (ROPE_THETA ** (jnp.arange(0, HEAD_DIM, 2, dtype=jnp.float32) / HEAD_DIM))
    ang = jnp.arange(seq, dtype=jnp.float32)[:, None] * inv[None, :]
    return jnp.cos(ang), jnp.sin(ang)


def apply_rope(t, cos, sin):
    shp = (1, t.shape[1]) + (1,) * (t.ndim - 3) + (HEAD_DIM // 2,)
    cs, sn = cos.reshape(shp), sin.reshape(shp)
    t1, t2 = jnp.split(t.astype(jnp.float32), 2, -1)
    return jnp.concatenate([t1 * cs - t2 * sn, t1 * sn + t2 * cs], -1).astype(t.dtype)


def neighbourhood_attention(q, k, v, rpb):
    B, S = q.shape[:2]
    rows = S // GRID_W
    wr = min(NA_WIN_ROWS, rows)
    to_grid = lambda t: t.reshape(B, rows, GRID_W, NA_HEADS, HEAD_DIM)
    qg, kg, vg = to_grid(q), to_grid(k), to_grid(v)
    r = jnp.arange(rows)
    row_start = jnp.clip(r - wr // 2, 0, rows - wr)
    key_rows = row_start[:, None] + jnp.arange(wr)[None, :]
    kb = kg[:, key_rows]
    vb = vg[:, key_rows]
    cidx = jnp.arange(GRID_W)
    col_start = jnp.clip(cidx - NA_WIN_COLS // 2, 0, GRID_W - NA_WIN_COLS)
    col_ok = (cidx[None, :] >= col_start[:, None]) & (cidx[None, :] < col_start[:, None] + NA_WIN_COLS)
    dr = key_rows - r[:, None]
    dc = cidx[None, :] - cidx[:, None]
    bias = rpb[:, dr + NA_WIN_ROWS - 1]
    bias = jnp.take(bias, jnp.clip(dc + NA_WIN_COLS - 1, 0, 2 * NA_WIN_COLS - 2), axis=-1)
    bias = bias.transpose(0, 1, 3, 2, 4).astype(jnp.float32)
    s = jnp.einsum('brqhd,brwkhd->bhrqwk', qg, kb).astype(jnp.float32) * (HEAD_DIM ** -0.5)
    s = jnp.where(col_ok[None, None, None, :, None, :], s + bias[None], NEG)
    p = jax.nn.softmax(s.reshape(s.shape[:4] + (wr * GRID_W,)), -1).reshape(s.shape).astype(v.dtype)
    o = jnp.einsum('bhrqwk,brwkhd->brqhd', p, vb)
    return o.reshape(B, S, A_WIDTH)


def diff_attention(q, k, v, lam, sub_g, lambda_init):
    B, S = q.shape[:2]
    nb = S // BLOCK
    qb = q.reshape(B, nb, BLOCK, DIFF_HEADS, 2, HEAD_DIM).transpose(1, 0, 2, 3, 4, 5)

    def one_block(qblk):
        s = jnp.einsum('bqhmd,bkhmd->bhmqk', qblk, k).astype(jnp.float32) * (HEAD_DIM ** -0.5)
        p = jax.nn.softmax(s, -1)
        w = (p[:, :, 0] - lam * p[:, :, 1]).astype(v.dtype)
        return jnp.einsum('bhqk,bkhe->bqhe', w, v)

    o = lax.map(one_block, qb)
    o = o.transpose(1, 0, 2, 3, 4).reshape(B, S, DIFF_HEADS, 2 * HEAD_DIM).astype(jnp.float32)
    o = o * lax.rsqrt(jnp.mean(o * o, -1, keepdims=True) + LN_EPS) * sub_g.astype(jnp.float32)
    return (o * (1.0 - lambda_init)).astype(q.dtype).reshape(B, S, B_V_WIDTH)


def sliding_window_gqa(q, k, v, sink):
    B, S = q.shape[:2]
    nb = S // BLOCK
    G = SWA_Q_HEADS // SWA_KV_HEADS
    qb = q.reshape(B, nb, BLOCK, SWA_KV_HEADS, G, HEAD_DIM)
    pad = lambda t: jnp.pad(t, ((0, 0), (BLOCK, BLOCK), (0, 0), (0, 0))).reshape(B, nb + 2, BLOCK, SWA_KV_HEADS, HEAD_DIM)
    band = lambda t: jnp.concatenate([t[:, i:i + nb] for i in range(3)], axis=2)
    kb, vb = band(pad(k)), band(pad(v))
    s = jnp.einsum('bnqhgd,bnkhd->bnhgqk', qb, kb).astype(jnp.float32) * (HEAD_DIM ** -0.5)
    blk = jnp.arange(nb)[:, None, None]
    qpos = blk * BLOCK + jnp.arange(BLOCK)[None, :, None]
    kpos = (blk - 1) * BLOCK + jnp.arange(3 * BLOCK)[None, None, :]
    ok = (jnp.abs(qpos - kpos) <= SWA_WINDOW) & (kpos >= 0) & (kpos < S)
    s = jnp.where(ok[None, :, None, None], s, NEG)
    sink_l = sink.astype(jnp.float32).reshape(1, 1, SWA_KV_HEADS, G, 1, 1)
    m = jnp.maximum(jnp.max(s, -1, keepdims=True), sink_l)
    e = jnp.exp(s - m)
    p = e / (jnp.sum(e, -1, keepdims=True) + jnp.exp(sink_l - m))
    o = jnp.einsum('bnhgqk,bnkhd->bnqhgd', p.astype(v.dtype), vb)
    return o.reshape(B, S, C_Q)


def peer(h, w_query, sub_keys, expert_u, expert_v):
    B, S, D = h.shape
    nb = S // BLOCK
    half = PEER_KEY_DIM // 2
    hb = h.reshape(B, nb, BLOCK, D).transpose(1, 0, 2, 3)

    def one_block(xb):
        q = jnp.einsum('btd,de->bte', xb, w_query).reshape(B, BLOCK, PEER_HEADS, 2, half)
        s = jnp.einsum('bthpe,hpne->bthpn', q, sub_keys).astype(jnp.float32)
        sv, si = lax.top_k(s, PEER_TOPK)
        cand = sv[..., 0, :, None] + sv[..., 1, None, :]
        cidx = si[..., 0, :, None] * PEER_N_KEYS + si[..., 1, None, :]
        fv, fpos = lax.top_k(cand.reshape(B, BLOCK, PEER_HEADS, PEER_TOPK * PEER_TOPK), PEER_TOPK)
        eidx = jnp.take_along_axis(cidx.reshape(B, BLOCK, PEER_HEADS, PEER_TOPK * PEER_TOPK), fpos, -1)
        g = jax.nn.softmax(fv, -1)
        u = expert_u[eidx]
        act = jax.nn.gelu(jnp.einsum('btd,bthkd->bthk', xb, u).astype(jnp.float32), approximate=False)
        w = (g * act).astype(xb.dtype)
        return jnp.einsum('bthk,bthkd->btd', w, expert_v[eidx])

    o = lax.map(one_block, hb)
    return o.transpose(1, 0, 2, 3).reshape(B, S, D)


def setup_inputs(seed: int = 0) -> dict:
    key = jax.random.key(seed)
    ks = jax.random.split(key, 24)
    D = D_MODEL
    nrm = lambda k, shape, s: jax.random.normal(k, shape, jnp.float32) * s
    x = nrm(ks[0], (BATCH, SEQ, D), 1.0)
    c = nrm(ks[1], (BATCH, D), 1.0)
    w_ada = nrm(ks[2], (DEPTH, 2, D, 3 * D), 0.1 * D ** -0.5)
    b_ada = jnp.concatenate([nrm(ks[3], (DEPTH, 2, 2 * D), 0.02), 1.0 + nrm(ks[4], (DEPTH, 2, D), 0.02)], -1)
    ln_g = 1.0 + nrm(ks[5], (DEPTH, 2, D), 0.02)
    ln_b = nrm(ks[6], (DEPTH, 2, D), 0.02)
    even_scale = jnp.asarray(np.concatenate([np.ones(2 * A_WIDTH), np.full(A_WIDTH, DEEPNORM_BETA),
                                             np.ones(2 * B_QK_WIDTH), np.full(B_V_WIDTH, DEEPNORM_BETA)]), jnp.float32)
    w_in_even = nrm(ks[7], (N_EVEN, D, EVEN_IN), D ** -0.5) * even_scale
    rpb = nrm(ks[8], (N_EVEN, NA_HEADS, 2 * NA_WIN_ROWS - 1, 2 * NA_WIN_COLS - 1), 0.02)
    lam_q1 = nrm(ks[9], (N_EVEN, HEAD_DIM), 0.1)
    lam_k1 = nrm(ks[10], (N_EVEN, HEAD_DIM), 0.1)
    lam_q2 = nrm(ks[11], (N_EVEN, HEAD_DIM), 0.1)
    lam_k2 = nrm(ks[12], (N_EVEN, HEAD_DIM), 0.1)
    diff_sub_g = 1.0 + nrm(ks[13], (N_EVEN, 2 * HEAD_DIM), 0.02)
    w_out_even = nrm(ks[14], (N_EVEN, D, D), DEEPNORM_BETA * D ** -0.5)
    odd_scale = jnp.asarray(np.concatenate([np.ones(C_Q + C_KV), np.full(C_KV, DEEPNORM_BETA)]), jnp.float32)
    w_in_odd = nrm(ks[15], (N_ODD, D, ODD_IN), D ** -0.5) * odd_scale
    sink = nrm(ks[16], (N_ODD, SWA_Q_HEADS), 0.5)
    w_out_odd = nrm(ks[17], (N_ODD, D, D), DEEPNORM_BETA * D ** -0.5)
    peer_w_query = nrm(ks[18], (DEPTH, D, PEER_HEADS * PEER_KEY_DIM), D ** -0.5)
    peer_sub_keys = nrm(ks[19], (DEPTH, PEER_HEADS, 2, PEER_N_KEYS, PEER_KEY_DIM // 2), (PEER_KEY_DIM // 2) ** -0.5)
    peer_u = nrm(ks[20], (DEPTH, PEER_N_EXPERTS, D), D ** -0.5)
    peer_v = nrm(ks[21], (DEPTH, PEER_N_EXPERTS, D), DEEPNORM_BETA * PEER_HEADS ** -0.5)
    return {"x": x, "c": c, "w_ada": w_ada, "b_ada": b_ada, "ln_g": ln_g, "ln_b": ln_b,
            "w_in_even": w_in_even, "rpb": rpb, "lam_q1": lam_q1, "lam_k1": lam_k1,
            "lam_q2": lam_q2, "lam_k2": lam_k2, "diff_sub_g": diff_sub_g, "w_out_even": w_out_even,
            "w_in_odd": w_in_odd, "sink": sink, "w_out_odd": w_out_odd,
            "peer_w_query": peer_w_query, "peer_sub_keys": peer_sub_keys, "peer_u": peer_u, "peer_v": peer_v}


def reference(x, c, w_ada, b_ada, ln_g, ln_b, w_in_even, rpb, lam_q1, lam_k1, lam_q2, lam_k2, diff_sub_g,
              w_out_even, w_in_odd, sink, w_out_odd, peer_w_query, peer_sub_keys, peer_u, peer_v):
    B, S, D = x.shape
    cos, sin = rope_tables(S)
    cond = jax.nn.silu(c.astype(jnp.float32)).astype(x.dtype)
    mods = jnp.einsum('bd,lsde->lsbe', cond, w_ada) + b_ada[:, :, None, :]
    even_splits = [A_WIDTH, 2 * A_WIDTH, 3 * A_WIDTH, 3 * A_WIDTH + B_QK_WIDTH, 3 * A_WIDTH + 2 * B_QK_WIDTH]
    odd_splits = [C_Q, C_Q + C_KV]
    for layer in range(DEPTH):
        shift, scale, gate = jnp.split(mods[layer, 0], 3, -1)
        h = x * (1.0 + scale[:, None]) + shift[:, None]
        if layer % 2 == 0:
            i = layer // 2
            proj = jnp.einsum('bsd,de->bse', h, w_in_even[i])
            qa, ka, va, qb, kb, vb = jnp.split(proj, even_splits, -1)
            hd4 = lambda t: t.reshape(B, S, NA_HEADS, HEAD_DIM)
            o_a = neighbourhood_attention(hd4(qa), hd4(ka), hd4(va), rpb[i])
            qb = apply_rope(qb.reshape(B, S, DIFF_HEADS, 2, HEAD_DIM), cos, sin)
            kb = apply_rope(kb.reshape(B, S, DIFF_HEADS, 2, HEAD_DIM), cos, sin)
            lambda_init = 0.8 - 0.6 * math.exp(-0.3 * layer)
            lam = (jnp.exp(jnp.sum(lam_q1[i].astype(jnp.float32) * lam_k1[i].astype(jnp.float32)))
                   - jnp.exp(jnp.sum(lam_q2[i].astype(jnp.float32) * lam_k2[i].astype(jnp.float32))) + lambda_init)
            o_b = diff_attention(qb, kb, vb.reshape(B, S, DIFF_HEADS, 2 * HEAD_DIM), lam, diff_sub_g[i], lambda_init)
            y = jnp.einsum('bse,ed->bsd', jnp.concatenate([o_a, o_b], -1), w_out_even[i])
        else:
            j = layer // 2
            proj = jnp.einsum('bsd,de->bse', h, w_in_odd[j])
            qc, kc, vc = jnp.split(proj, odd_splits, -1)
            qc = apply_rope(qc.reshape(B, S, SWA_Q_HEADS, HEAD_DIM), cos, sin)
            kc = apply_rope(kc.reshape(B, S, SWA_KV_HEADS, HEAD_DIM), cos, sin)
            o_c = sliding_window_gqa(qc, kc, vc.reshape(B, S, SWA_KV_HEADS, HEAD_DIM), sink[j])
            y = jnp.einsum('bse,ed->bsd', o_c, w_out_odd[j])
        x = layer_norm(DEEPNORM_ALPHA * x + gate[:, None] * y, ln_g[layer, 0], ln_b[layer, 0])
        shift2, scale2, gate2 = jnp.split(mods[layer, 1], 3, -1)
        h2 = x * (1.0 + scale2[:, None]) + shift2[:, None]
        y2 = peer(h2, peer_w_query[layer], peer_sub_keys[layer], peer_u[layer], peer_v[layer])
        x = layer_norm(DEEPNORM_ALPHA * x + gate2[:, None] * y2, ln_g[layer, 1], ln_b[layer, 1])
    return x
```

```python
import math
from contextlib import ExitStack

import numpy as np
import ml_dtypes
import concourse.bass as bass
import concourse.mybir as mybir
from concourse.bass_utils import run_bass_kernel_spmd

F32 = mybir.dt.float32
BF16 = mybir.dt.bfloat16
U32 = mybir.dt.uint32
I32 = mybir.dt.int32
U16 = mybir.dt.uint16
AF = mybir.ActivationFunctionType
ALU = mybir.AluOpType
AX = mybir.AxisListType


class R:
    __slots__ = ("name", "w", "rs")

    def __init__(self, name):
        self.name = name
        self.w = None
        self.rs = []


class Op:
    __slots__ = ("eng", "fn", "deps", "idx", "sig", "is_dma", "sem", "val", "n", "epoch", "cnt", "name")


class Tl:
    __slots__ = ("ap", "r", "t")

    def __init__(self, t, r):
        self.t = t
        self.ap = t.ap() if hasattr(t, "ap") and callable(t.ap) else t
        self.r = r


EPOCH = 16000
NRING = {"sp": 8, "pool": 8, "act": 4}
ENGS = ["pe", "act", "dve", "pool", "sp"]


class Prog:
    def __init__(self, nc, es):
        self.nc = nc
        self.es = es
        self.ops = {e: [] for e in ENGS}
        self.dma_cnt = {q: 0 for q in NRING}
        self.dma_hist = {q: [] for q in NRING}
        self.dma_tot = {}
        self.out_r = R("out")
        self.n_sb = 0

    def sb(self, name, shape, dtype):
        t = self.es.enter_context(self.nc.sbuf_tensor(name, list(shape), dtype))
        return Tl(t, R(name))

    def ps(self, name, shape, dtype):
        t = self.es.enter_context(self.nc.psum_tensor(name, list(shape), dtype))
        return Tl(t, R(name))

    def _mk(self, eng, fn, r, w, is_dma, n, name):
        o = Op()
        o.eng = eng
        o.fn = fn
        o.is_dma = is_dma
        o.n = n
        o.sig = False
        o.name = name
        o.sem = None
        o.val = 0
        deps = []
        seen = set()
        for x in r:
            if x.w is not None and id(x.w) not in seen:
                seen.add(id(x.w))
                deps.append((x.w, "raw"))
        for x in w:
            if x.w is not None and id(x.w) not in seen:
                seen.add(id(x.w))
                deps.append((x.w, "waw"))
            for rd in x.rs:
                if id(rd) not in seen:
                    seen.add(id(rd))
                    deps.append((rd, "war"))
        o.deps = deps
        for x in w:
            x.w = o
            x.rs = []
        for x in r:
            if x.w is not o:
                x.rs.append(o)
        o.idx = len(self.ops[eng])
        self.ops[eng].append(o)
        return o

    def op(self, eng, fn, r=(), w=(), name=""):
        return self._mk(eng, fn, r, w, False, 1, name)

    def dma(self, q, fn, r=(), w=(), n=1, name=""):
        o = self._mk(q, fn, r, w, True, n, name)
        k = self.dma_cnt[q]
        self.dma_cnt[q] += 1
        slot = k % NRING[q]
        tot = self.dma_tot.get((q, slot), 0) + 16 * n
        self.dma_tot[(q, slot)] = tot
        o.sem = (q, slot)
        o.val = tot
        hist = self.dma_hist[q]
        if k >= NRING[q]:
            o.deps.append((hist[k - NRING[q]], "ring"))
        hist.append(o)
        return o

    def emit(self):
        nc = self.nc
        es = self.es
        need = {}
        for e in ENGS:
            known = {f: -1 for f in ENGS}
            known_dma = {}
            for o in self.ops[e]:
                waits = []
                for d, kind in o.deps:
                    if d.is_dma:
                        if known_dma.get(d.sem, 0) >= d.val:
                            continue
                        known_dma[d.sem] = d.val
                        waits.append(d)
                    else:
                        if d.eng == e and not o.is_dma:
                            if kind != "raw" or e == "pe":
                                continue
                        if known[d.eng] >= d.idx:
                            continue
                        known[d.eng] = d.idx
                        d.sig = True
                        waits.append(d)
                need[id(o)] = waits
        nep = {}
        for e in ENGS:
            c = 0
            for o in self.ops[e]:
                if o.is_dma:
                    continue
                if o.sig:
                    o.epoch = c // EPOCH
                    o.cnt = c % EPOCH + 1
                    c += 1
            nep[e] = c // EPOCH + 1
        sems = {}
        for e in ENGS:
            for k in range(nep[e]):
                sems[(e, k)] = es.enter_context(nc.semaphore(f"s_{e}_{k}"))
        dsems = {}
        for q in NRING:
            for s in range(NRING[q]):
                if (q, s) in self.dma_tot:
                    dsems[(q, s)] = es.enter_context(nc.semaphore(f"d_{q}_{s}"))
        block = es.enter_context(nc.Block())

        def run(e):
            def body(eng):
                for o in self.ops[e]:
                    for d in need[id(o)]:
                        if d.is_dma:
                            eng.wait_ge(dsems[d.sem], d.val)
                        else:
                            eng.wait_ge(sems[(d.eng, d.epoch)], d.cnt)
                    ins = o.fn(eng)
                    if o.is_dma:
                        if o.n == 1:
                            ins.then_inc(dsems[o.sem], 16)
                        else:
                            for i_ in ins:
                                i_.then_inc(dsems[o.sem], 16)
                    elif o.sig:
                        ins.then_inc(sems[(e, o.epoch)], 1)
                if e in NRING:
                    for s in range(NRING[e]):
                        if (e, s) in self.dma_tot:
                            eng.wait_ge(dsems[(e, s)], self.dma_tot[(e, s)])
            return body

        block.tensor(run("pe"))
        block.scalar(run("act"))
        block.vector(run("dve"))
        block.gpsimd(run("pool"))
        block.sync(run("sp"))


def make_identity(P, ident):
    def f(e):
        e.memset(ident.ap[:], 1.0)
        return e.affine_select(out=ident.ap[:], in_=ident.ap[:], pattern=[[-1, 128]],
                               compare_op=ALU.is_equal, fill=0.0, base=0, channel_multiplier=1)
    P.op("pool", f, w=[ident.r], name="ident")


class Arena:
    def __init__(self, P, name, n, dtype):
        self.tl = P.sb(name, [128, n], dtype)
        self.n = n
        self.off = 0
        self.name = name
        self.k = 0

    def reset(self):
        self.off = 0

    def get(self, shape, name=None):
        sz = 1
        for s in shape[1:]:
            sz *= s
        assert self.off + sz <= self.n, (self.name, self.off, sz, self.n)
        ap = self.tl.ap[0:shape[0], self.off:self.off + sz]
        self.off += sz
        if len(shape) == 3:
            ap = ap.rearrange("p (a b) -> p a b", b=shape[2])
        elif len(shape) == 4:
            ap = ap.rearrange("p (a b c) -> p a b c", b=shape[2], c=shape[3])
        self.k += 1
        t = Tl.__new__(Tl)
        t.t = None
        t.ap = ap
        t.r = R(name or f"{self.name}{self.k}")
        return t


def barrier(P):
    lasts = []
    for e in ENGS:
        for o in reversed(P.ops[e]):
            if not o.is_dma:
                lasts.append(o)
                break
    for q in NRING:
        hist = P.dma_hist[q]
        lasts.extend(hist[-NRING[q]:])
    for e in ENGS:
        o = P.op(e, lambda eng: eng.nop(), name="barrier")
        o.deps = [(d, "raw") for d in lasts if d is not o]


ALPHA = (2 * 4) ** 0.25
LN_EPS = 1e-5


def ln_epilogue(P, y_ps, x_t, gate_b, g_b, b_b, tmp_a, tmp_b, st, eps_t):
    P.op("dve", lambda e: e.tensor_tensor(out=tmp_a.ap, in0=y_ps.ap, in1=gate_b.ap, op=ALU.mult),
         r=[y_ps.r, gate_b.r], w=[tmp_a.r])
    P.op("dve", lambda e: e.scalar_tensor_tensor(out=tmp_b.ap, in0=x_t.ap, scalar=ALPHA, in1=tmp_a.ap,
                                                 op0=ALU.mult, op1=ALU.add),
         r=[x_t.r, tmp_a.r], w=[tmp_b.r])
    ln_only(P, tmp_b, x_t, g_b, b_b, tmp_a, st, eps_t)


def ln_only(P, z, x_out, g_b, b_b, tmp, st, eps_t):
    def stats(e):
        e.bn_stats(out=st.ap[:, 0:6], in_=z.ap[:, 0:512])
        return e.bn_stats(out=st.ap[:, 6:12], in_=z.ap[:, 512:1024])
    P.op("dve", stats, r=[z.r], w=[st.r])
    P.op("dve", lambda e: e.bn_aggr(out=st.ap[:, 12:14], in_=st.ap[:, 0:12]), r=[st.r], w=[st.r])
    P.op("act", lambda e: e.activation(out=st.ap[:, 14:15], in_=st.ap[:, 13:14], func=AF.Sqrt,
                                       bias=eps_t.ap[:, 0:1], scale=1.0), r=[st.r, eps_t.r], w=[st.r])
    P.op("dve", lambda e: e.reciprocal(out=st.ap[:, 15:16], in_=st.ap[:, 14:15]), r=[st.r], w=[st.r])
    P.op("dve", lambda e: e.tensor_scalar(out=tmp.ap, in0=z.ap, scalar1=st.ap[:, 12:13], scalar2=st.ap[:, 15:16],
                                          op0=ALU.subtract, op1=ALU.mult), r=[z.r, st.r], w=[tmp.r])
    P.op("dve", lambda e: e.tensor_tensor(out=tmp.ap, in0=tmp.ap, in1=g_b.ap, op=ALU.mult),
         r=[tmp.r, g_b.r], w=[tmp.r])
    P.op("dve", lambda e: e.tensor_tensor(out=x_out.ap, in0=tmp.ap, in1=b_b.ap, op=ALU.add),
         r=[tmp.r, b_b.r], w=[x_out.r])


def load_w_bf16(P, dst, w_dram, ncols, cchunk=1024):
    for c0 in range(0, ncols, cchunk):
        c1 = min(ncols, c0 + cchunk)
        P.dma("pool", lambda e, c0=c0, c1=c1: e.dma_start(
            out=dst.ap[:, :, c0:c1], in_=w_dram[:, c0:c1].rearrange("(kc p) n -> p kc n", p=128)),
            w=[dst.r])


def bcast_load(P, q, dst, row_ap):
    P.dma(q, lambda e: e.dma_start(out=dst.ap, in_=row_ap.partition_broadcast(128)), w=[dst.r])


def modulate_transpose(P, x_t, scale_b, shift_b, tmp, hb, hT, ps_t, ident, h_f32=None):
    P.op("dve", lambda e: e.scalar_tensor_tensor(out=tmp.ap, in0=scale_b.ap, scalar=1.0, in1=x_t.ap,
                                                 op0=ALU.add, op1=ALU.mult),
         r=[scale_b.r, x_t.r], w=[tmp.r])
    if h_f32 is not None:
        P.op("dve", lambda e: e.tensor_tensor(out=h_f32.ap, in0=tmp.ap, in1=shift_b.ap, op=ALU.add),
             r=[tmp.r, shift_b.r], w=[h_f32.r])
        P.op("act", lambda e: e.copy(out=hb.ap, in_=h_f32.ap), r=[h_f32.r], w=[hb.r])
    else:
        P.op("dve", lambda e: e.tensor_tensor(out=hb.ap, in0=tmp.ap, in1=shift_b.ap, op=ALU.add),
             r=[tmp.r, shift_b.r], w=[hb.r])

    def tr(e):
        for k in range(8):
            ins = e.transpose(out=ps_t.ap[:, k, :], in_=hb.ap[:, k * 128:(k + 1) * 128], identity=ident.ap)
        return ins
    P.op("pe", tr, r=[hb.r, ident.r], w=[ps_t.r])
    P.op("act", lambda e: e.copy(out=hT.ap, in_=ps_t.ap), r=[ps_t.r], w=[hT.r])


def build_A(N, rope_c0, rope_nh, qscale_chunks):
    nc = bass.Bass("TRN2", target_bir_lowering=False)
    x = nc.dram_tensor("x", [2048, 1024], F32, kind="ExternalInput").ap()
    cT = nc.dram_tensor("cT", [128, 8], F32, kind="ExternalInput").ap()
    w_ada = nc.dram_tensor("w_ada", [2, 1024, 3072], F32, kind="ExternalInput").ap()
    b_ada = nc.dram_tensor("b_ada", [2, 3072], F32, kind="ExternalInput").ap()
    w_in = nc.dram_tensor("w_in", [1024, N], F32, kind="ExternalInput").ap()
    cos = nc.dram_tensor("cos", [2048, 32], F32, kind="ExternalInput").ap()
    sin = nc.dram_tensor("sin", [2048, 32], F32, kind="ExternalInput").ap()
    proj = nc.dram_tensor("proj", [2048, N], BF16, kind="ExternalOutput").ap()
    mods_o = nc.dram_tensor("mods", [2, 3072], F32, kind="ExternalOutput").ap()
    with ExitStack() as es:
        P = Prog(nc, es)
        ident = P.sb("ident", [128, 128], BF16)
        make_identity(P, ident)
        cond = P.sb("cond", [128, 8], F32)
        cond_rep = P.sb("cond_rep", [128, 8, 128], F32)
        bada = [P.sb(f"bada{s}", [128, 3072], F32) for s in range(2)]
        mods = [P.sb(f"mods{s}", [128, 3072], F32) for s in range(2)]
        wa = [P.sb(f"wa{i}", [128, 8, 512], F32) for i in range(2)]
        ps_m = [P.ps(f"ps_m{i}", [128, 512], F32) for i in range(2)]
        ps_t = P.ps("ps_t", [128, 8, 128], BF16)
        wb = P.sb("wb", [128, 8, N], BF16)
        P.dma("sp", lambda e: e.dma_start(out=cond.ap, in_=cT), w=[cond.r])
        for s in range(2):
            bcast_load(P, "sp", bada[s], b_ada[s, :])
        load_w_bf16(P, wb, w_in, N)
        P.op("act", lambda e: e.activation(out=cond.ap, in_=cond.ap, func=AF.Silu), r=[cond.r], w=[cond.r])
        P.op("dve", lambda e: e.tensor_copy(out=cond_rep.ap, in_=cond.ap.unsqueeze(2).to_broadcast([128, 8, 128])),
             r=[cond.r], w=[cond_rep.r])
        k = 0
        for s in range(2):
            for n in range(6):
                wt = wa[k % 2]
                pm = ps_m[k % 2]
                k += 1
                P.dma("sp", lambda e, wt=wt, s=s, n=n: e.dma_start(
                    out=wt.ap, in_=w_ada[s, :, n * 512:(n + 1) * 512].rearrange("(kc p) n -> p kc n", p=128)),
                    w=[wt.r])

                def mm(e, wt=wt, pm=pm):
                    for kc in range(8):
                        ins = e.matmul(pm.ap, lhsT=cond_rep.ap[:, kc, :], rhs=wt.ap[:, kc, :],
                                       start=(kc == 0), stop=(kc == 7))
                    return ins
                P.op("pe", mm, r=[cond_rep.r, wt.r], w=[pm.r])
                P.op("dve", lambda e, pm=pm, s=s, n=n: e.tensor_tensor(
                    out=mods[s].ap[:, n * 512:(n + 1) * 512], in0=pm.ap, in1=bada[s].ap[:, n * 512:(n + 1) * 512],
                    op=ALU.add), r=[pm.r, bada[s].r], w=[mods[s].r])
            P.dma("sp", lambda e, s=s: e.dma_start(out=mods_o[s:s + 1, :], in_=mods[s].ap[0:1, :]),
                  r=[mods[s].r], w=[P.out_r])
        shift_b = Tl.__new__(Tl); shift_b.ap = mods[0].ap[:, 0:1024]; shift_b.r = mods[0].r
        scale_b = Tl.__new__(Tl); scale_b.ap = mods[0].ap[:, 1024:2048]; scale_b.r = mods[0].r
        xt = [P.sb(f"xt{i}", [128, 1024], F32) for i in range(2)]
        tmp = P.sb("tmp", [128, 1024], F32)
        hb = P.sb("hb", [128, 1024], BF16)
        hT = P.sb("hT", [128, 8, 128], BF16)
        stage = P.sb("stage", [128, N], F32)
        ob = [P.sb(f"ob{i}", [128, N], BF16) for i in range(2)]
        cs = [P.sb(f"cs{i}", [128, 2, 32], F32) for i in range(2)]
        ra = P.sb("ra", [128, rope_nh, 32], F32)
        rb = P.sb("rb", [128, rope_nh, 32], F32)
        rc = P.sb("rc", [128, rope_nh, 32], F32)
        rd = P.sb("rd", [128, rope_nh, 32], F32)
        nch = N // 512
        for t in range(16):
            x_t = xt[t % 2]
            o_t = ob[t % 2]
            c_t = cs[t % 2]
            P.dma("sp", lambda e, x_t=x_t, t=t: e.dma_start(out=x_t.ap, in_=x[t * 128:(t + 1) * 128, :]), w=[x_t.r])
            P.dma("sp", lambda e, c_t=c_t, t=t: e.dma_start(out=c_t.ap[:, 0, :], in_=cos[t * 128:(t + 1) * 128, :]), w=[c_t.r])
            P.dma("sp", lambda e, c_t=c_t, t=t: e.dma_start(out=c_t.ap[:, 1, :], in_=sin[t * 128:(t + 1) * 128, :]), w=[c_t.r])
            modulate_transpose(P, x_t, scale_b, shift_b, tmp, hb, hT, ps_t, ident)
            for n in range(nch):
                pm = ps_m[n % 2]

                def mm(e, pm=pm, n=n):
                    for kc in range(8):
                        ins = e.matmul(pm.ap, lhsT=hT.ap[:, kc, :], rhs=wb.ap[:, kc, n * 512:(n + 1) * 512],
                                       start=(kc == 0), stop=(kc == 7))
                    return ins
                P.op("pe", mm, r=[hT.r, wb.r], w=[pm.r])
                sc = 0.125 if n in qscale_chunks else 1.0
                P.op("act", lambda e, pm=pm, n=n, sc=sc: e.activation(
                    out=stage.ap[:, n * 512:(n + 1) * 512], in_=pm.ap, func=AF.Copy, scale=sc),
                    r=[pm.r], w=[stage.r])
            c1 = rope_c0 + rope_nh * 64
            V = stage.ap[:, rope_c0:c1].rearrange("p (h two d) -> p h two d", two=2, d=32)
            O = o_t.ap[:, rope_c0:c1].rearrange("p (h two d) -> p h two d", two=2, d=32)
            cosb = c_t.ap[:, 0:1, :].to_broadcast([128, rope_nh, 32])
            sinb = c_t.ap[:, 1:2, :].to_broadcast([128, rope_nh, 32])
            P.op("dve", lambda e, V=V, cosb=cosb: e.tensor_tensor(out=ra.ap, in0=V[:, :, 0, :], in1=cosb, op=ALU.mult),
                 r=[stage.r, c_t.r], w=[ra.r])
            P.op("dve", lambda e, V=V, sinb=sinb: e.tensor_tensor(out=rb.ap, in0=V[:, :, 1, :], in1=sinb, op=ALU.mult),
                 r=[stage.r, c_t.r], w=[rb.r])
            P.op("pool", lambda e, V=V, sinb=sinb: e.tensor_tensor(out=rc.ap, in0=V[:, :, 0, :], in1=sinb, op=ALU.mult),
                 r=[stage.r, c_t.r], w=[rc.r])
            P.op("pool", lambda e, V=V, cosb=cosb: e.tensor_tensor(out=rd.ap, in0=V[:, :, 1, :], in1=cosb, op=ALU.mult),
                 r=[stage.r, c_t.r], w=[rd.r])
            P.op("dve", lambda e, O=O: e.tensor_tensor(out=O[:, :, 0, :], in0=ra.ap, in1=rb.ap, op=ALU.subtract),
                 r=[ra.r, rb.r], w=[o_t.r])
            P.op("dve", lambda e, O=O: e.tensor_tensor(out=O[:, :, 1, :], in0=rc.ap, in1=rd.ap, op=ALU.add),
                 r=[rc.r, rd.r], w=[o_t.r])
            if rope_c0 > 0:
                P.op("act", lambda e, o_t=o_t: e.copy(out=o_t.ap[:, 0:rope_c0], in_=stage.ap[:, 0:rope_c0]),
                     r=[stage.r], w=[o_t.r])
            if c1 < N:
                P.op("act", lambda e, o_t=o_t, c1=c1: e.copy(out=o_t.ap[:, c1:N], in_=stage.ap[:, c1:N]),
                     r=[stage.r], w=[o_t.r])
            P.dma("sp", lambda e, o_t=o_t, t=t: e.dma_start(out=proj[t * 128:(t + 1) * 128, :], in_=o_t.ap),
                  r=[o_t.r], w=[P.out_r])
        P.emit()
    return nc


def view(tl, ap):
    t = Tl.__new__(Tl)
    t.t = None
    t.ap = ap
    t.r = tl.r
    return t


def build_B(kind, lambda_init=0.0, do_attn=True, do_k3=True, do_peer=True, n_tiles=16):
    even = kind == "even"
    nc = bass.Bass("TRN2", target_bir_lowering=False)

    def din(name, shape, dt=F32):
        return nc.dram_tensor(name, list(shape), dt, kind="ExternalInput").ap()
    x = din("x", [2048, 1024])
    mods = din("mods", [2, 3072])
    w_out = din("w_out", [1024, 1024])
    ln_g = din("ln_g", [2, 1024])
    ln_b = din("ln_b", [2, 1024])
    wq = din("wq", [1024, 2048])
    skT = din("skT", [16, 128, 128])
    pu = din("pu", [16384, 1024])
    pv = din("pv", [16384, 1024])
    iota16 = din("iota16", [128, 16])
    if even:
        qaT = din("qaT", [8, 64, 2048], BF16)
        kaT = din("kaT", [8, 64, 2688], BF16)
        va = din("va", [8, 2688, 64], BF16)
        bias_na = din("bias_na", [8, 25, 128, 128], BF16)
        qdT = din("qdT", [4, 128, 2048], BF16)
        kdT = din("kdT", [4, 128, 8192], BF16)
        vd = din("vd", [4, 8192, 128], BF16)
        lamv = din("lamv", [4, 64])
        subg = din("subg", [128])
        lamc = din("lamc", [2])
    else:
        qcT = din("qcT", [4, 64, 8192], BF16)
        kcT = din("kcT", [4, 64, 2304], BF16)
        vc = din("vc", [4, 2304, 64], BF16)
        bias_swa = din("bias_swa", [9, 128, 512], BF16)
        sink = din("sink", [16])
    out = nc.dram_tensor("out", [2048, 1024], F32, kind="ExternalOutput").ap()
    x1_d = nc.dram_tensor("x1_d", [2048, 1024], F32, kind="ExternalOutput").ap()
    attn_dbg = nc.dram_tensor("attn_dbg", [2048, 1024], BF16, kind="ExternalOutput").ap()

    with ExitStack() as es:
        P = Prog(nc, es)
        ident = P.sb("ident", [128, 128], BF16)
        make_identity(P, ident)
        eps_t = P.sb("eps_t", [128, 1], F32)
        P.op("pool", lambda e: e.memset(eps_t.ap, LN_EPS), w=[eps_t.r])
        AF32 = Arena(P, "af32", 22 * 1024, F32)
        ABF = Arena(P, "abf", 36 * 1024, BF16)
        psA = [P.ps(f"psA{i}", [128, 1024], F32) for i in range(2)]
        bank = []
        for i in range(4):
            b = Tl.__new__(Tl)
            b.t = None
            b.ap = psA[i // 2].ap[:, (i % 2) * 512:(i % 2 + 1) * 512]
            b.r = R(f"bank{i}")
            bank.append(b)
        psO = [P.ps(f"psO{i}", [128, 512], F32) for i in range(2)]
        psT = P.ps("psT", [128, 8, 128], BF16)
        x1r = [R(f"x1d{i}") for i in range(16)]

        attn = ABF.get([128, 16, 1024], "attn")
        abf_mark = ABF.off

        if do_attn and even:
            lv = AF32.get([128, 4, 64], "lv")
            lsm = AF32.get([128, 8], "lsm")
            junk64 = AF32.get([128, 64], "junk64")
            subg_b = AF32.get([128, 128], "subg_b")
            P.dma("sp", lambda e: e.dma_start(out=lv.ap, in_=lamv.rearrange("a d -> (a d)").partition_broadcast(128).rearrange("p (a d) -> p a d", d=64)), w=[lv.r])
            bcast_load(P, "sp", subg_b, subg)
            lamc_b = AF32.get([128, 2], "lamc_b")
            bcast_load(P, "sp", lamc_b, lamc)
            for k_ in range(2):
                P.op("dve", lambda e, k_=k_: e.scalar_tensor_tensor(
                    out=junk64.ap, in0=lv.ap[:, 2 * k_, :], scalar=1.0, in1=lv.ap[:, 2 * k_ + 1, :],
                    op0=ALU.mult, op1=ALU.mult, accum_out=lsm.ap[:, k_:k_ + 1]), r=[lv.r], w=[junk64.r, lsm.r])
            P.op("act", lambda e: e.activation(out=lsm.ap[:, 2:4], in_=lsm.ap[:, 0:2], func=AF.Exp), r=[lsm.r], w=[lsm.r])
            P.op("dve", lambda e: e.tensor_scalar(out=lsm.ap[:, 4:5], in0=lsm.ap[:, 2:3], scalar1=lsm.ap[:, 3:4],
                                                  scalar2=lamc_b.ap[:, 0:1], op0=ALU.subtract, op1=ALU.add),
                 r=[lsm.r, lamc_b.r], w=[lsm.r])
            P.op("dve", lambda e: e.tensor_scalar(out=lsm.ap[:, 5:6], in0=lsm.ap[:, 4:5], scalar1=-1.0, scalar2=None,
                                                  op0=ALU.mult), r=[lsm.r], w=[lsm.r])
            P.op("dve", lambda e: e.tensor_scalar(out=subg_b.ap, in0=subg_b.ap, scalar1=lamc_b.ap[:, 1:2],
                                                  scalar2=None, op0=ALU.mult), r=[subg_b.r, lamc_b.r], w=[subg_b.r])
            af_mark = AF32.off
            n_kT = ABF.get([64, 2688], "na_kT")
            n_qT = ABF.get([64, 2048], "na_qT")
            n_v_sb = ABF.get([128, 21, 65], "na_v")
            n_bias = ABF.get([128, 25, 128], "na_bias")
            n_PT = [ABF.get([128, 5, 128], f"na_PT{i}") for i in range(2)]
            rz1 = [AF32.get([128, 1], f"rz1_{i}") for i in range(2)]
            for h in range(8):
                P.dma("sp", lambda e, h=h: e.dma_start(out=n_kT.ap, in_=kaT[h]), w=[n_kT.r])
                P.dma("sp", lambda e, h=h: e.dma_start(out=n_qT.ap, in_=qaT[h]), w=[n_qT.r])
                P.op("pool", lambda e: e.memset(n_v_sb.ap, 1.0), w=[n_v_sb.r])
                P.dma("sp", lambda e, h=h: e.dma_start(out=n_v_sb.ap[:, :, 0:64], in_=va[h].rearrange("(t p) d -> p t d", p=128)), w=[n_v_sb.r])
                P.dma("sp", lambda e, h=h: e.dma_start(out=n_bias.ap, in_=bias_na[h].rearrange("s p q -> p s q")), w=[n_bias.r])
                for i in range(n_tiles):
                    slot = {0: 0, 1: 1, 14: 3, 15: 4}.get(i, 2)
                    S = psA[i % 2]
                    Sr = [bank[(i % 2) * 2].r, bank[(i % 2) * 2 + 1].r]
                    pt = n_PT[i % 2]

                    def qk(e, S=S, i=i, slot=slot):
                        for j in range(5):
                            e.matmul(S.ap[:, j * 128:(j + 1) * 128], lhsT=n_kT.ap[:, (i + j) * 128:(i + j + 1) * 128],
                                     rhs=n_qT.ap[:, i * 128:(i + 1) * 128], start=True, stop=False)
                            ins = e.matmul(S.ap[:, j * 128:(j + 1) * 128], lhsT=ident.ap, rhs=n_bias.ap[:, slot * 5 + j, :],
                                           start=False, stop=True)
                        return ins
                    P.op("pe", qk, r=[n_kT.r, n_qT.r, n_bias.r, ident.r], w=Sr)
                    P.op("act", lambda e, S=S, pt=pt: e.activation(
                        out=pt.ap, in_=S.ap[:, 0:640].rearrange("p (j q) -> p j q", q=128), func=AF.Exp),
                        r=Sr, w=[pt.r])
                    O = psO[i % 2]

                    def pvm(e, O=O, pt=pt, i=i):
                        for j in range(5):
                            ins = e.matmul(O.ap[:, 0:65], lhsT=pt.ap[:, j, :], rhs=n_v_sb.ap[:, i + j, :],
                                           start=(j == 0), stop=(j == 4))
                        return ins
                    P.op("pe", pvm, r=[pt.r, n_v_sb.r], w=[O.r])
                    rz = rz1[i % 2]
                    P.op("dve", lambda e, O=O, rz=rz: e.reciprocal(out=rz.ap, in_=O.ap[:, 64:65]), r=[O.r], w=[rz.r])
                    P.op("dve", lambda e, O=O, rz=rz, i=i, h=h: e.tensor_scalar(
                        out=attn.ap[:, i, h * 64:(h + 1) * 64], in0=O.ap[:, 0:64], scalar1=rz.ap[:, 0:1], scalar2=None,
                        op0=ALU.mult), r=[O.r, rz.r], w=[attn.r])
            barrier(P)
            ABF.off = abf_mark
            AF32.off = af_mark
            d_kT = ABF.get([128, 8192], "d_kT")
            d_qT = ABF.get([128, 2048], "d_qT")
            d_v_sb = ABF.get([128, 64, 129], "d_v")
            d_PT = [ABF.get([128, 256], f"d_PT{i}") for i in range(2)]
            Oe = AF32.get([128, 2, 2, 129], "Oe")
            rzd = AF32.get([128, 8], "rzd")
            o1 = AF32.get([128, 128], "o1")
            o2 = AF32.get([128, 128], "o2")
            junk = AF32.get([128, 128], "junkd")
            for h in range(4):
                for c4 in range(4):
                    P.dma("sp", lambda e, h=h, c4=c4: e.dma_start(out=d_kT.ap[:, c4 * 2048:(c4 + 1) * 2048],
                                                                   in_=kdT[h][:, c4 * 2048:(c4 + 1) * 2048]), w=[d_kT.r])
                P.dma("sp", lambda e, h=h: e.dma_start(out=d_qT.ap, in_=qdT[h]), w=[d_qT.r])
                P.op("pool", lambda e: e.memset(d_v_sb.ap, 1.0), w=[d_v_sb.r])
                for c4 in range(4):
                    P.dma("sp", lambda e, h=h, c4=c4: e.dma_start(
                        out=d_v_sb.ap[:, c4 * 16:(c4 + 1) * 16, 0:128],
                        in_=vd[h][c4 * 2048:(c4 + 1) * 2048, :].rearrange("(t p) d -> p t d", p=128)), w=[d_v_sb.r])
                for qb in range(n_tiles // 2):
                    for m in range(2):
                        for kt in range(64):
                            S = bank[kt % 4]
                            pt = d_PT[kt % 2]
                            P.op("pe", lambda e, S=S, m=m, kt=kt, qb=qb: e.matmul(
                                S.ap[:, 0:256], lhsT=d_kT.ap[m * 64:(m + 1) * 64, kt * 128:(kt + 1) * 128],
                                rhs=d_qT.ap[m * 64:(m + 1) * 64, qb * 256:(qb + 1) * 256], start=True, stop=True),
                                r=[d_kT.r, d_qT.r], w=[S.r])
                            P.op("act", lambda e, S=S, pt=pt: e.activation(out=pt.ap, in_=S.ap[:, 0:256], func=AF.Exp),
                                 r=[S.r], w=[pt.r])

                            def pvm(e, pt=pt, kt=kt):
                                for qs in range(2):
                                    ins = e.matmul(psO[qs].ap[:, 0:129], lhsT=pt.ap[:, qs * 128:(qs + 1) * 128],
                                                   rhs=d_v_sb.ap[:, kt, :], start=(kt == 0), stop=(kt == 63))
                                return ins
                            P.op("pe", pvm, r=[pt.r, d_v_sb.r], w=[psO[0].r, psO[1].r])
                        for qs in range(2):
                            P.op("act", lambda e, m=m, qs=qs: e.copy(out=Oe.ap[:, m, qs, :], in_=psO[qs].ap[:, 0:129]),
                                 r=[psO[qs].r], w=[Oe.r])
                    for qs in range(2):
                        i = qb * 2 + qs
                        P.op("dve", lambda e, qs=qs: e.reciprocal(out=rzd.ap[:, 0:2], in_=Oe.ap[:, :, qs, 128]),
                             r=[Oe.r], w=[rzd.r])
                        P.op("dve", lambda e: e.tensor_tensor(out=rzd.ap[:, 2:3], in0=rzd.ap[:, 1:2], in1=lsm.ap[:, 5:6],
                                                              op=ALU.mult), r=[rzd.r, lsm.r], w=[rzd.r])
                        P.op("dve", lambda e, qs=qs: e.tensor_scalar(out=o1.ap, in0=Oe.ap[:, 0, qs, 0:128], scalar1=rzd.ap[:, 0:1],
                                                                     scalar2=None, op0=ALU.mult), r=[Oe.r, rzd.r], w=[o1.r])
                        P.op("dve", lambda e, qs=qs: e.scalar_tensor_tensor(
                            out=o2.ap, in0=Oe.ap[:, 1, qs, 0:128], scalar=rzd.ap[:, 2:3], in1=o1.ap,
                            op0=ALU.mult, op1=ALU.add), r=[Oe.r, rzd.r, o1.r], w=[o2.r])
                        P.op("dve", lambda e: e.scalar_tensor_tensor(out=junk.ap, in0=o2.ap, scalar=1.0, in1=o2.ap, op0=ALU.mult, op1=ALU.mult, accum_out=rzd.ap[:, 3:4]), r=[o2.r], w=[junk.r, rzd.r])
                        P.op("act", lambda e: e.activation(out=rzd.ap[:, 4:5], in_=rzd.ap[:, 3:4], func=AF.Sqrt,
                                                           bias=eps_t.ap[:, 0:1], scale=1.0 / 128.0),
                             r=[rzd.r, eps_t.r], w=[rzd.r])
                        P.op("dve", lambda e: e.reciprocal(out=rzd.ap[:, 5:6], in_=rzd.ap[:, 4:5]), r=[rzd.r], w=[rzd.r])
                        P.op("dve", lambda e, i=i, h=h: e.scalar_tensor_tensor(
                            out=attn.ap[:, i, 512 + h * 128:512 + (h + 1) * 128], in0=o2.ap, scalar=rzd.ap[:, 5:6],
                            in1=subg_b.ap, op0=ALU.mult, op1=ALU.mult), r=[o2.r, rzd.r, subg_b.r], w=[attn.r])
            barrier(P)
        elif do_attn:
            esink = AF32.get([128, 16], "esink")
            bcast_load(P, "sp", esink, sink)
            P.op("act", lambda e: e.activation(out=esink.ap, in_=esink.ap, func=AF.Exp), r=[esink.r], w=[esink.r])
            c_kT = ABF.get([64, 2304], "c_kT")
            c_qT = ABF.get([64, 16, 512], "c_qT")
            c_v_sb = ABF.get([128, 18, 65], "c_v")
            c_bias = ABF.get([128, 9, 512], "c_bias")
            c_PT = [ABF.get([128, 512], f"c_PT{i}") for i in range(6)]
            den = [AF32.get([128, 8], f"den{i}") for i in range(2)]
            P.dma("sp", lambda e: e.dma_start(out=c_bias.ap, in_=bias_swa.rearrange("s p q -> p s q")), w=[c_bias.r])
            kk = 0
            for hk in range(4):
                P.dma("sp", lambda e, hk=hk: e.dma_start(out=c_kT.ap, in_=kcT[hk]), w=[c_kT.r])
                P.dma("sp", lambda e, hk=hk: e.dma_start(out=c_qT.ap, in_=qcT[hk].rearrange("d (t q) -> d t q", q=512)), w=[c_qT.r])
                P.op("pool", lambda e: e.memset(c_v_sb.ap, 1.0), w=[c_v_sb.r])
                P.dma("sp", lambda e, hk=hk: e.dma_start(out=c_v_sb.ap[:, :, 0:64], in_=vc[hk].rearrange("(t p) d -> p t d", p=128)), w=[c_v_sb.r])
                for i in range(n_tiles):
                    slot = 0 if i == 0 else (2 if i == 15 else 1)
                    pts = []
                    for j in range(3):
                        S = bank[kk % 4]
                        pt = c_PT[kk % 6]
                        kk += 1
                        pts.append(pt)

                        def qk(e, S=S, i=i, j=j, slot=slot):
                            e.matmul(S.ap, lhsT=c_kT.ap[:, (i + j) * 128:(i + j + 1) * 128], rhs=c_qT.ap[:, i, :],
                                     start=True, stop=False)
                            return e.matmul(S.ap, lhsT=ident.ap, rhs=c_bias.ap[:, slot * 3 + j, :], start=False, stop=True)
                        P.op("pe", qk, r=[c_kT.r, c_qT.r, c_bias.r, ident.r], w=[S.r])
                        P.op("act", lambda e, S=S, pt=pt: e.activation(out=pt.ap, in_=S.ap, func=AF.Exp), r=[S.r], w=[pt.r])
                    O = psO[i % 2]
                    O4 = O.ap[:, 0:260].rearrange("p (g d) -> p g d", d=65)

                    def pvm(e, O4=O4, pts=pts, i=i):
                        for g in range(4):
                            for j in range(3):
                                ins = e.matmul(O4[:, g, :], lhsT=pts[j].ap[:, g * 128:(g + 1) * 128], rhs=c_v_sb.ap[:, i + j, :],
                                               start=(j == 0), stop=(j == 2))
                        return ins
                    P.op("pe", pvm, r=[p_.r for p_ in pts] + [c_v_sb.r], w=[O.r])
                    dn = den[i % 2]
                    P.op("dve", lambda e, O4=O4, dn=dn, hk=hk: e.tensor_tensor(
                        out=dn.ap[:, 0:4], in0=O4[:, :, 64], in1=esink.ap[:, hk * 4:(hk + 1) * 4], op=ALU.add),
                        r=[O.r, esink.r], w=[dn.r])
                    P.op("dve", lambda e, dn=dn: e.reciprocal(out=dn.ap[:, 4:8], in_=dn.ap[:, 0:4]), r=[dn.r], w=[dn.r])
                    P.op("dve", lambda e, O4=O4, dn=dn, i=i, hk=hk: e.tensor_tensor(
                        out=attn.ap[:, i, hk * 256:(hk + 1) * 256].rearrange("p (g d) -> p g d", d=64),
                        in0=O4[:, :, 0:64], in1=dn.ap[:, 4:8].unsqueeze(2).to_broadcast([128, 4, 64]), op=ALU.mult),
                        r=[O.r, dn.r], w=[attn.r])
            barrier(P)
        ABF.off = abf_mark
        AF32.reset()
        if do_attn:
            for i in range(n_tiles):
                P.dma("sp", lambda e, i=i: e.dma_start(out=attn_dbg[i * 128:(i + 1) * 128, :], in_=attn.ap[:, i, :]),
                      r=[attn.r], w=[P.out_r])

        if do_k3:
            if not do_attn:
                for i in range(n_tiles):
                    P.dma("sp", lambda e, i=i: e.dma_start(out=attn.ap[:, i, :], in_=x[i * 128:(i + 1) * 128, :].bitcast(BF16)[:, 0:1024]),
                          w=[attn.r])
            wo = ABF.get([128, 8, 1024], "wo")
            aT = ABF.get([128, 8, 128], "aT")
            load_w_bf16(P, wo, w_out, 1024)
            gate1 = AF32.get([128, 1024], "gate1")
            g0 = AF32.get([128, 1024], "g0")
            b0 = AF32.get([128, 1024], "b0")
            bcast_load(P, "sp", gate1, mods[0, 2048:3072])
            bcast_load(P, "sp", g0, ln_g[0, :])
            bcast_load(P, "sp", b0, ln_b[0, :])
            xt = [AF32.get([128, 1024], f"xt{i}") for i in range(2)]
            ta = AF32.get([128, 1024], "ta")
            tb = AF32.get([128, 1024], "tb")
            st = AF32.get([128, 16], "st")
            for i in range(n_tiles):
                x_t = xt[i % 2]
                P.dma("sp", lambda e, x_t=x_t, i=i: e.dma_start(out=x_t.ap, in_=x[i * 128:(i + 1) * 128, :]), w=[x_t.r])

                def tr(e, i=i):
                    for k in range(8):
                        ins = e.transpose(out=psT.ap[:, k, :], in_=attn.ap[:, i, k * 128:(k + 1) * 128], identity=ident.ap)
                    return ins
                P.op("pe", tr, r=[attn.r, ident.r], w=[psT.r])
                P.op("act", lambda e: e.copy(out=aT.ap, in_=psT.ap), r=[psT.r], w=[aT.r])
                Y = psA[i % 2]
                Yr = [bank[(i % 2) * 2].r, bank[(i % 2) * 2 + 1].r]

                def mm(e, Y=Y):
                    for n in range(2):
                        for kc in range(8):
                            ins = e.matmul(Y.ap[:, n * 512:(n + 1) * 512], lhsT=aT.ap[:, kc, :], rhs=wo.ap[:, kc, n * 512:(n + 1) * 512],
                                           start=(kc == 0), stop=(kc == 7))
                    return ins
                P.op("pe", mm, r=[aT.r, wo.r], w=Yr)
                yv = Tl.__new__(Tl); yv.t = None; yv.ap = Y.ap; yv.r = Yr[0]
                P.op("dve", lambda e, Y=Y: e.tensor_tensor(out=ta.ap, in0=Y.ap, in1=gate1.ap, op=ALU.mult),
                     r=Yr + [gate1.r], w=[ta.r])
                P.op("dve", lambda e, x_t=x_t: e.scalar_tensor_tensor(out=tb.ap, in0=x_t.ap, scalar=ALPHA, in1=ta.ap,
                                                                     op0=ALU.mult, op1=ALU.add), r=[x_t.r, ta.r], w=[tb.r])
                ln_only(P, tb, x_t, g0, b0, ta, st, eps_t)
                dst = x1_d if do_peer else out
                P.dma("sp", lambda e, x_t=x_t, i=i, dst=dst: e.dma_start(out=dst[i * 128:(i + 1) * 128, :], in_=x_t.ap),
                      r=[x_t.r], w=[x1r[i]])
            barrier(P)
        ABF.reset()
        AF32.reset()

        if do_peer:
            src = x1_d if do_k3 else x
            wqb = ABF.get([128, 8, 2048], "wqb")
            skb = ABF.get([128, 16, 128], "skb")
            hb = ABF.get([128, 1024], "hb")
            hT = ABF.get([128, 8, 128], "hT")
            qTs = [ABF.get([128, 128], f"qTs{i}") for i in range(2)]
            load_w_bf16(P, wqb, wq, 2048)
            P.dma("pool", lambda e: e.dma_start(out=skb.ap, in_=skT.rearrange("hp e n -> e hp n")), w=[skb.r])
            scale2 = AF32.get([128, 1024], "scale2")
            shift2 = AF32.get([128, 1024], "shift2")
            gate2 = AF32.get([128, 1024], "gate2")
            g1 = AF32.get([128, 1024], "g1")
            b1 = AF32.get([128, 1024], "b1")
            bcast_load(P, "sp", shift2, mods[1, 0:1024])
            bcast_load(P, "sp", scale2, mods[1, 1024:2048])
            bcast_load(P, "sp", gate2, mods[1, 2048:3072])
            bcast_load(P, "sp", g1, ln_g[1, :])
            bcast_load(P, "sp", b1, ln_b[1, :])
            io16 = AF32.get([128, 16], "io16")
            P.dma("sp", lambda e: e.dma_start(out=io16.ap, in_=iota16), w=[io16.r])
            x_t = AF32.get([128, 1024], "px")
            h2 = AF32.get([128, 1024], "h2")
            tmp = AF32.get([128, 1024], "ptmp")
            acc = AF32.get([128, 1024], "acc")
            G = [AF32.get([128, 1024], f"G{i}") for i in range(3)]
            s_all = AF32.get([128, 16, 128], "s_all")
            cand = AF32.get([128, 8, 256], "cand")
            wk = AF32.get([128, 256], "wk")
            sv = AF32.get([128, 16, 16], "sv")
            si = AF32.get([128, 16, 16], "si")
            sif = AF32.get([128, 16, 16], "sif")
            fv = AF32.get([128, 8, 16], "fv")
            fpos = AF32.get([128, 8, 16], "fpos")
            ab_u = AF32.get([128, 2, 128], "ab_u")
            ab_f = AF32.get([128, 2, 128], "ab_f")
            oh = AF32.get([128, 8, 16, 16], "oh")
            iab = AF32.get([128, 2, 128], "iab")
            eidx_f = AF32.get([128, 128], "eidx_f")
            eidx = AF32.get([128, 128], "eidx")
            gts = AF32.get([128, 8, 16], "gts")
            zs = AF32.get([128, 16], "zs")
            actv = AF32.get([128, 128], "actv")
            wgt = AF32.get([128, 128], "wgt")
            st = AF32.get([128, 16], "pst")
            si_u = si.ap.bitcast(U32)
            fpos_u = fpos.ap.bitcast(U32)
            ab_uu = ab_u.ap.bitcast(U32)
            eidx_u = eidx.ap.bitcast(U32)
            sv4 = sv.ap.rearrange("p (h two) k -> p h two k", two=2)
            sif4 = sif.ap.rearrange("p (h two) k -> p h two k", two=2)
            kq = 0
            for i in range(n_tiles):
                P.dma("sp", lambda e, i=i: e.dma_start(out=x_t.ap, in_=src[i * 128:(i + 1) * 128, :]), r=[x1r[i]], w=[x_t.r])
                modulate_transpose(P, x_t, scale2, shift2, tmp, hb, hT, psT, ident, h_f32=h2)
                for hp in range(16):
                    qps = bank[kq % 4]
                    sps = bank[(kq + 2) % 4]
                    qs_ = qTs[kq % 2]
                    kq += 1

                    def qm(e, qps=qps, hp=hp):
                        for kc in range(8):
                            ins = e.matmul(qps.ap[:, 0:128], lhsT=wqb.ap[:, kc, hp * 128:(hp + 1) * 128], rhs=hT.ap[:, kc, :],
                                           start=(kc == 0), stop=(kc == 7))
                        return ins
                    P.op("pe", qm, r=[wqb.r, hT.r], w=[qps.r])
                    P.op("act", lambda e, qps=qps, qs_=qs_: e.copy(out=qs_.ap, in_=qps.ap[:, 0:128]), r=[qps.r], w=[qs_.r])
                    P.op("pe", lambda e, sps=sps, qs_=qs_, hp=hp: e.matmul(sps.ap[:, 0:128], lhsT=qs_.ap, rhs=skb.ap[:, hp, :],
                                                                          start=True, stop=True), r=[qs_.r, skb.r], w=[sps.r])
                    P.op("act", lambda e, sps=sps, hp=hp: e.copy(out=s_all.ap[:, hp, :], in_=sps.ap[:, 0:128]),
                         r=[sps.r], w=[s_all.r])
                for hp in range(16):
                    P.op("dve", lambda e, hp=hp: e.max(out=sv.ap[:, hp, 0:8], in_=s_all.ap[:, hp, :]), r=[s_all.r], w=[sv.r])
                    P.op("dve", lambda e, hp=hp: e.match_replace(out=wk.ap[:, 0:128], in_to_replace=sv.ap[:, hp, 0:8],
                                                                 in_values=s_all.ap[:, hp, :], imm_value=-1e30),
                         r=[sv.r, s_all.r], w=[wk.r])
                    P.op("dve", lambda e, hp=hp: e.max(out=sv.ap[:, hp, 8:16], in_=wk.ap[:, 0:128]), r=[wk.r], w=[sv.r])
                    P.op("dve", lambda e, hp=hp: e.max_index(out=si_u[:, hp, 0:8], in_max=sv.ap[:, hp, 0:8],
                                                             in_values=s_all.ap[:, hp, :]), r=[sv.r, s_all.r], w=[si.r])
                    P.op("dve", lambda e, hp=hp: e.max_index(out=si_u[:, hp, 8:16], in_max=sv.ap[:, hp, 8:16],
                                                             in_values=wk.ap[:, 0:128]), r=[sv.r, wk.r], w=[si.r])
                P.op("dve", lambda e: e.tensor_copy(out=sif.ap, in_=si_u), r=[si.r], w=[sif.r])
                P.op("dve", lambda e: e.tensor_tensor(
                    out=cand.ap.rearrange("p h (a b) -> p h a b", b=16),
                    in0=sv4[:, :, 0, :].unsqueeze(3).to_broadcast([128, 8, 16, 16]),
                    in1=sv4[:, :, 1, :].unsqueeze(2).to_broadcast([128, 8, 16, 16]), op=ALU.add), r=[sv.r], w=[cand.r])
                for h in range(8):
                    P.op("dve", lambda e, h=h: e.max(out=fv.ap[:, h, 0:8], in_=cand.ap[:, h, :]), r=[cand.r], w=[fv.r])
                    P.op("dve", lambda e, h=h: e.match_replace(out=wk.ap, in_to_replace=fv.ap[:, h, 0:8],
                                                               in_values=cand.ap[:, h, :], imm_value=-1e30),
                         r=[fv.r, cand.r], w=[wk.r])
                    P.op("dve", lambda e, h=h: e.max(out=fv.ap[:, h, 8:16], in_=wk.ap), r=[wk.r], w=[fv.r])
                    P.op("dve", lambda e, h=h: e.max_index(out=fpos_u[:, h, 0:8], in_max=fv.ap[:, h, 0:8],
                                                           in_values=cand.ap[:, h, :]), r=[fv.r, cand.r], w=[fpos.r])
                    P.op("dve", lambda e, h=h: e.max_index(out=fpos_u[:, h, 8:16], in_max=fv.ap[:, h, 8:16],
                                                           in_values=wk.ap), r=[fv.r, wk.r], w=[fpos.r])
                fpos_flat = fpos_u.rearrange("p h k -> p (h k)")
                P.op("dve", lambda e: e.tensor_single_scalar(out=ab_uu[:, 0, :], in_=fpos_flat, scalar=4,
                                                             op=ALU.logical_shift_right), r=[fpos.r], w=[ab_u.r])
                P.op("dve", lambda e: e.tensor_single_scalar(out=ab_uu[:, 1, :], in_=fpos_flat, scalar=15,
                                                             op=ALU.bitwise_and), r=[fpos.r], w=[ab_u.r])
                P.op("dve", lambda e: e.tensor_copy(out=ab_f.ap, in_=ab_uu), r=[ab_u.r], w=[ab_f.r])
                for p_ in range(2):
                    abv = ab_f.ap[:, p_, :].rearrange("p (h k) -> p h k", k=16)
                    P.op("dve", lambda e, abv=abv: e.tensor_tensor(
                        out=oh.ap, in0=abv.unsqueeze(3).to_broadcast([128, 8, 16, 16]),
                        in1=io16.ap.unsqueeze(1).unsqueeze(1).to_broadcast([128, 8, 16, 16]), op=ALU.is_equal),
                        r=[ab_f.r, io16.r], w=[oh.r])
                    P.op("dve", lambda e, p_=p_: e.tensor_tensor(
                        out=oh.ap, in0=oh.ap, in1=sif4[:, :, p_, :].unsqueeze(2).to_broadcast([128, 8, 16, 16]), op=ALU.mult),
                        r=[oh.r, sif.r], w=[oh.r])
                    P.op("dve", lambda e, p_=p_: e.tensor_reduce(
                        out=iab.ap[:, p_, :].rearrange("p (h k) -> p h k", k=16), in_=oh.ap, axis=AX.X, op=ALU.add),
                        r=[oh.r], w=[iab.r])
                P.op("dve", lambda e: e.scalar_tensor_tensor(out=eidx_f.ap, in0=iab.ap[:, 0, :], scalar=128.0, in1=iab.ap[:, 1, :],
                                                             op0=ALU.mult, op1=ALU.add), r=[iab.r], w=[eidx_f.r])
                P.op("dve", lambda e: e.tensor_copy(out=eidx_u, in_=eidx_f.ap), r=[eidx_f.r], w=[eidx.r])
                P.op("dve", lambda e: e.tensor_tensor(out=gts.ap, in0=fv.ap, in1=fv.ap[:, :, 0:1].to_broadcast([128, 8, 16]),
                                                      op=ALU.subtract), r=[fv.r], w=[gts.r])
                P.op("act", lambda e: e.activation(out=gts.ap, in_=gts.ap, func=AF.Exp), r=[gts.r], w=[gts.r])
                P.op("dve", lambda e: e.tensor_reduce(out=zs.ap[:, 0:8], in_=gts.ap, axis=AX.X, op=ALU.add), r=[gts.r], w=[zs.r])
                P.op("dve", lambda e: e.reciprocal(out=zs.ap[:, 8:16], in_=zs.ap[:, 0:8]), r=[zs.r], w=[zs.r])
                P.op("dve", lambda e: e.tensor_tensor(out=gts.ap, in0=gts.ap, in1=zs.ap[:, 8:16].unsqueeze(2).to_broadcast([128, 8, 16]),
                                                      op=ALU.mult), r=[gts.r, zs.r], w=[gts.r])
                for j in range(128):
                    g_ = G[j % 3]
                    P.dma("pool", lambda e, g_=g_, j=j: e.indirect_dma_start(
                        out=g_.ap, out_offset=None, in_=pu,
                        in_offset=bass.IndirectOffsetOnAxis(ap=eidx_u[:, j:j + 1], axis=0)), r=[eidx.r], w=[g_.r])
                    P.op("dve", lambda e, g_=g_, j=j: e.scalar_tensor_tensor(out=tmp.ap, in0=g_.ap, scalar=1.0, in1=h2.ap, op0=ALU.mult, op1=ALU.mult, accum_out=actv.ap[:, j:j + 1]), r=[g_.r, h2.r], w=[tmp.r, actv.r])
                P.op("act", lambda e: e.activation(out=wgt.ap, in_=actv.ap, func=AF.Gelu), r=[actv.r], w=[wgt.r])
                P.op("dve", lambda e: e.tensor_tensor(out=wgt.ap, in0=wgt.ap, in1=gts.ap.rearrange("p h k -> p (h k)"),
                                                      op=ALU.mult), r=[wgt.r, gts.r], w=[wgt.r])
                for j in range(128):
                    g_ = G[(j + 2) % 3]
                    P.dma("pool", lambda e, g_=g_, j=j: e.indirect_dma_start(
                        out=g_.ap, out_offset=None, in_=pv,
                        in_offset=bass.IndirectOffsetOnAxis(ap=eidx_u[:, j:j + 1], axis=0)), r=[eidx.r], w=[g_.r])
                    if j == 0:
                        P.op("dve", lambda e, g_=g_: e.tensor_scalar(out=acc.ap, in0=g_.ap, scalar1=wgt.ap[:, 0:1], scalar2=None,
                                                                     op0=ALU.mult), r=[g_.r, wgt.r], w=[acc.r])
                    else:
                        P.op("dve", lambda e, g_=g_, j=j: e.scalar_tensor_tensor(
                            out=acc.ap, in0=g_.ap, scalar=wgt.ap[:, j:j + 1], in1=acc.ap, op0=ALU.mult, op1=ALU.add),
                            r=[g_.r, wgt.r, acc.r], w=[acc.r])
                ln_epilogue(P, acc, x_t, gate2, g1, b1, tmp, h2, st, eps_t)
                P.dma("sp", lambda e, i=i: e.dma_start(out=out[i * 128:(i + 1) * 128, :], in_=x_t.ap), r=[x_t.r], w=[P.out_r])
        P.emit()
    return nc


BF = ml_dtypes.bfloat16
NEG = -1e30
S_LEN = 8192


def rope_tables_np():
    inv = (1.0 / (10000.0 ** (np.arange(0, 64, 2, dtype=np.float32) / np.float32(64)))).astype(np.float32)
    ang = np.arange(S_LEN, dtype=np.float32)[:, None] * inv[None, :]
    return np.cos(ang).astype(np.float32), np.sin(ang).astype(np.float32)


def na_block_map(qtr):
    g0 = qtr * 16
    bm = [g0 - 2 + p for p in range(21)]
    bm = [b if 0 <= b < 64 else -1 for b in bm]
    if qtr == 0:
        bm[0] = 3
    if qtr == 3:
        bm[19] = 60
    return bm


def gather_blocks(a, bm):
    out = np.zeros((len(bm) * 128,) + a.shape[1:], a.dtype)
    for p, b in enumerate(bm):
        if b >= 0:
            out[p * 128:(p + 1) * 128] = a[b * 128:(b + 1) * 128]
    return out


def na_bias_tables(rpb_l, qtr):
    out = np.full((8, 5, 640, 128), NEG, np.float32)
    bm = na_block_map(qtr)
    qq = np.arange(128)
    for slot, i in enumerate((0, 1, 2, 14, 15)):
        gi = qtr * 16 + i
        r0 = 2 * gi
        r = r0 + qq // 64
        cq = qq % 64
        blk = np.repeat(np.array(bm[i:i + 5]), 128)
        tok = blk * 128 + np.tile(np.arange(128), 5)
        valid = blk >= 0
        kr = np.where(valid, tok // 64, 0)
        ck = np.where(valid, tok % 64, 0)
        rs = np.clip(r - 4, 0, 128 - 8)
        cs = np.clip(cq - 8, 0, 64 - 16)
        ok = (valid[:, None] & (kr[:, None] >= rs[None, :]) & (kr[:, None] < rs[None, :] + 8)
              & (ck[:, None] >= cs[None, :]) & (ck[:, None] < cs[None, :] + 16))
        dr = np.clip(kr[:, None] - r[None, :] + 7, 0, 14)
        dc = np.clip(ck[:, None] - cq[None, :] + 15, 0, 30)
        g = rpb_l[:, dr, dc]
        out[:, slot] = np.where(ok[None], g, np.float32(NEG))
    return np.ascontiguousarray(out.reshape(8, 25, 128, 128)).astype(BF)


def swa_bias_tables(qtr):
    out = np.zeros((3, 3, 128, 4, 128), np.float32)
    kk = np.arange(128)
    qq = np.arange(128)
    for slot, i in enumerate((0, 1, 15)):
        gi = qtr * 16 + i
        for j in range(3):
            qpos = gi * 128 + qq
            kpos = (gi - 1 + j) * 128 + kk
            ok = (np.abs(qpos[None, :] - kpos[:, None]) <= 128) & (kpos[:, None] >= 0) & (kpos[:, None] < S_LEN)
            out[slot, j] = np.where(ok, np.float32(0), np.float32(NEG))[:, None, :]
    return np.ascontiguousarray(out.reshape(9, 128, 512)).astype(BF)


def pad_tokens(a, lo, hi):
    S = a.shape[0]
    out = np.zeros((hi - lo,) + a.shape[1:], a.dtype)
    s0, s1 = max(lo, 0), min(hi, S)
    out[s0 - lo:s1 - lo] = a[s0:s1]
    return out


def prep_B(layer, c, x_full, proj_full, mods_c, inp):
    b, qtr = c // 4, c % 4
    t0 = qtr * 2048
    m = {"x": np.ascontiguousarray(x_full[b, t0:t0 + 2048]), "mods": mods_c,
         "ln_g": np.ascontiguousarray(inp["ln_g"][layer]), "ln_b": np.ascontiguousarray(inp["ln_b"][layer]),
         "wq": np.ascontiguousarray(inp["peer_w_query"][layer]),
         "skT": np.ascontiguousarray(inp["peer_sub_keys"][layer].reshape(16, 128, 128).transpose(0, 2, 1)),
         "pu": np.ascontiguousarray(inp["peer_u"][layer]), "pv": np.ascontiguousarray(inp["peer_v"][layer]),
         "iota16": np.ascontiguousarray(np.broadcast_to(np.arange(16, dtype=np.float32), (128, 16)))}
    pb = proj_full[b]
    if layer % 2 == 0:
        i = layer // 2
        m["w_out"] = np.ascontiguousarray(inp["w_out_even"][i])
        qa = pb[t0:t0 + 2048, 0:512].reshape(2048, 8, 64)
        m["qaT"] = np.ascontiguousarray(qa.transpose(1, 2, 0))
        ka = gather_blocks(pb[:, 512:1024], na_block_map(qtr)).reshape(2688, 8, 64)
        m["kaT"] = np.ascontiguousarray(ka.transpose(1, 2, 0))
        va_ = gather_blocks(pb[:, 1024:1536], na_block_map(qtr)).reshape(2688, 8, 64)
        m["va"] = np.ascontiguousarray(va_.transpose(1, 0, 2))
        m["bias_na"] = na_bias_tables(inp["rpb"][i], qtr)
        qd = pb[t0:t0 + 2048, 1536:2048].reshape(2048, 4, 128)
        m["qdT"] = np.ascontiguousarray(qd.transpose(1, 2, 0))
        kd = pb[:, 2048:2560].reshape(8192, 4, 128)
        m["kdT"] = np.ascontiguousarray(kd.transpose(1, 2, 0))
        vd_ = pb[:, 2560:3072].reshape(8192, 4, 128)
        m["vd"] = np.ascontiguousarray(vd_.transpose(1, 0, 2))
        m["lamv"] = np.ascontiguousarray(np.stack([inp["lam_q1"][i], inp["lam_k1"][i], inp["lam_q2"][i], inp["lam_k2"][i]]))
        m["subg"] = np.ascontiguousarray(inp["diff_sub_g"][i])
        li = 0.8 - 0.6 * math.exp(-0.3 * layer)
        m["lamc"] = np.array([li, 1.0 - li], np.float32)
    else:
        j = layer // 2
        m["w_out"] = np.ascontiguousarray(inp["w_out_odd"][j])
        qc = pb[t0:t0 + 2048, 0:1024].reshape(16, 128, 4, 4, 64)
        m["qcT"] = np.ascontiguousarray(qc.transpose(2, 4, 0, 3, 1).reshape(4, 64, 8192))
        kc = pad_tokens(pb[:, 1024:1280], t0 - 128, t0 + 2048 + 128).reshape(2304, 4, 64)
        m["kcT"] = np.ascontiguousarray(kc.transpose(1, 2, 0))
        vc_ = pad_tokens(pb[:, 1280:1536], t0 - 128, t0 + 2048 + 128).reshape(2304, 4, 64)
        m["vc"] = np.ascontiguousarray(vc_.transpose(1, 0, 2))
        m["bias_swa"] = swa_bias_tables(qtr)
        m["sink"] = np.ascontiguousarray(inp["sink"][j])
    return m


_NC_CACHE = {}


def get_nc(key):
    if key not in _NC_CACHE:
        if key[0] == "A":
            _NC_CACHE[key] = build_A(3072, 1536, 16, (0, 3)) if key[1] == "even" else build_A(1536, 0, 20, (0, 1))
        else:
            _NC_CACHE[key] = build_B(key[1], key[2])
    return _NC_CACHE[key]


def kernel(**inp):
    inp = {k: np.asarray(v) for k, v in inp.items()}
    cosT, sinT = rope_tables_np()
    x = inp["x"].astype(np.float32)
    cores = list(range(8))
    for layer in range(4):
        kind = "even" if layer % 2 == 0 else "odd"
        ncA = get_nc(("A", kind))
        w_in = inp["w_in_even"][layer // 2] if kind == "even" else inp["w_in_odd"][layer // 2]
        in_maps = []
        for c in cores:
            b, q = c // 4, c % 4
            in_maps.append({"x": np.ascontiguousarray(x[b, q * 2048:(q + 1) * 2048]),
                            "cT": np.ascontiguousarray(inp["c"][b].reshape(8, 128).T),
                            "w_ada": np.ascontiguousarray(inp["w_ada"][layer]),
                            "b_ada": np.ascontiguousarray(inp["b_ada"][layer]),
                            "w_in": np.ascontiguousarray(w_in),
                            "cos": np.ascontiguousarray(cosT[q * 2048:(q + 1) * 2048]),
                            "sin": np.ascontiguousarray(sinT[q * 2048:(q + 1) * 2048])})
        res = run_bass_kernel_spmd(ncA, in_maps, core_ids=cores)
        proj = np.stack([np.asarray(res.results[c]["proj"]) for c in cores])
        proj = proj.reshape(2, 8192, proj.shape[-1])
        mods = [np.ascontiguousarray(np.asarray(res.results[c]["mods"])) for c in cores]
        ncB = get_nc(("B", kind, 0.0))
        in_maps = [prep_B(layer, c, x, proj, mods[c], inp) for c in cores]
        res = run_bass_kernel_spmd(ncB, in_maps, core_ids=cores)
        x = np.stack([np.asarray(res.results[c]["out"]) for c in cores]).reshape(2, 8192, 1024).astype(np.float32)
    return x


def build_fused(NT, n_layers=4, use_cc=False, dbg=False):
    T = NT * 128
    nc = bass.Bass("TRN2", target_bir_lowering=False)

    def din(name, shape, dt=F32):
        return nc.dram_tensor(name, list(shape), dt, kind="ExternalInput").ap()

    def dint(name, shape, dt=F32):
        return nc.dram_tensor(name, list(shape), dt, kind="Internal").ap()
    x_in = din("x", [T, 1024])
    cT = din("cT", [128, 8])
    w_ada = din("w_ada", [4, 2, 1024, 3072])
    b_ada = din("b_ada", [4, 2, 3072])
    w_in_e = din("w_in_even", [2, 1024, 3072])
    w_in_o = din("w_in_odd", [2, 1024, 1536])
    cos = din("cos", [T, 32])
    sin = din("sin", [T, 32])
    w_out_e = din("w_out_even", [2, 1024, 1024])
    w_out_o = din("w_out_odd", [2, 1024, 1024])
    ln_g = din("ln_g", [4, 2, 1024])
    ln_b = din("ln_b", [4, 2, 1024])
    wq_all = din("wq", [4, 1024, 2048])
    skT_all = din("skT", [4, 16, 128, 128])
    pu_all = din("pu", [4 * 16384, 1024])
    pv_all = din("pv", [4 * 16384, 1024])
    iota16 = din("iota16", [128, 16])
    eoff = din("eoff", [128, 4])
    bias_na = din("bias_na", [2, 8, 25, 128, 128], BF16)
    lamv = din("lamv", [2, 4, 64])
    subg = din("subg", [2, 128])
    lamc = din("lamc", [2, 2])
    bias_swa = din("bias_swa", [9, 128, 128], BF16)
    sink = din("sink", [2, 16])
    NKA = NT + 5
    NKC = NT + 2
    NKD = 64
    idx_na = din("idx_na", [128, 8 * NKA], U32)
    idx_d = din("idx_d", [128, 8 * NKD], U32)
    idx_c = din("idx_c", [128, 4 * NKC], U32)
    out = nc.dram_tensor("out", [T, 1024], F32, kind="ExternalOutput").ap()
    x1_d = dint("x1_d", [T, 1024])
    x2_d = [dint(f"x2_d{i}", [T, 1024]) for i in range(2)]
    mods_d = dint("mods_d", [2, 3072])
    attn_d = dint("attn_d", [T, 1024], BF16)
    attn_v = attn_d.rearrange("(t p) c -> p t c", p=128)
    proj_loc = dint("proj_loc", [24 * T, 128], BF16)
    proj_all = proj_loc
    assert not use_cc
    dbg_o = {}
    if dbg:
        dbg_o["attn"] = nc.dram_tensor("attn_dbg", [n_layers, T, 1024], BF16, kind="ExternalOutput").ap()
        dbg_o["x1"] = nc.dram_tensor("x1_dbg", [n_layers, T, 1024], F32, kind="ExternalOutput").ap()
        dbg_o["x2"] = nc.dram_tensor("x2_dbg", [n_layers, T, 1024], F32, kind="ExternalOutput").ap()

    with ExitStack() as es:
        P = Prog(nc, es)
        ident = P.sb("ident", [128, 128], BF16)
        make_identity(P, ident)
        eps_t = P.sb("eps_t", [128, 1], F32)
        P.op("pool", lambda e: e.memset(eps_t.ap, LN_EPS), w=[eps_t.r])
        cond = P.sb("cond", [128, 8], F32)
        cond_rep = P.sb("cond_rep", [128, 8, 128], F32)
        P.dma("sp", lambda e: e.dma_start(out=cond.ap, in_=cT), w=[cond.r])
        P.op("act", lambda e: e.activation(out=cond.ap, in_=cond.ap, func=AF.Silu), r=[cond.r], w=[cond.r])
        P.op("dve", lambda e: e.tensor_copy(out=cond_rep.ap, in_=cond.ap.unsqueeze(2).to_broadcast([128, 8, 128])),
             r=[cond.r], w=[cond_rep.r])
        ina = P.sb("ina", [128, 8 * NKA], U32)
        ind = P.sb("ind", [128, 8 * NKD], U32)
        inc = P.sb("inc", [128, 4 * NKC], U32)
        P.dma("sp", lambda e: e.dma_start(out=ina.ap, in_=idx_na), w=[ina.r])
        P.dma("sp", lambda e: e.dma_start(out=ind.ap, in_=idx_d), w=[ind.r])
        P.dma("sp", lambda e: e.dma_start(out=inc.ap, in_=idx_c), w=[inc.r])
        io16 = P.sb("io16", [128, 16], F32)
        P.dma("sp", lambda e: e.dma_start(out=io16.ap, in_=iota16), w=[io16.r])
        eoff_t = P.sb("eoff_t", [128, 4], F32)
        P.dma("sp", lambda e: e.dma_start(out=eoff_t.ap, in_=eoff), w=[eoff_t.r])
        AF32 = Arena(P, "af32", 24 * 1024, F32)
        ABF = Arena(P, "abf", 44 * 1024, BF16)
        psA = [P.ps(f"psA{i}", [128, 1024], F32) for i in range(2)]
        bank = []
        for i in range(4):
            b = Tl.__new__(Tl)
            b.t = None
            b.ap = psA[i // 2].ap[:, (i % 2) * 512:(i % 2 + 1) * 512]
            b.r = R(f"bank{i}")
            bank.append(b)
        psO = [P.ps(f"psO{i}", [128, 512], F32) for i in range(2)]
        psT = P.ps("psT", [128, 8, 128], BF16)
        x1r = [R(f"x1d{i}") for i in range(NT)]
        x2r = [[R(f"x2d{k}_{i}") for i in range(NT)] for k in range(2)]
        projr = R("proj")
        modsr = R("modsd")
        attnr = R("attn_d")
        proj_v = proj_loc.rearrange("(j t) c -> t j c", t=T)

        def gather(dst_ap, dst_r, idx_tl, col, extra_w=()):
            P.dma("pool", lambda e: e.indirect_dma_start(
                out=dst_ap, out_offset=None, in_=proj_all,
                in_offset=bass.IndirectOffsetOnAxis(ap=idx_tl.ap[:, col:col + 1], axis=0)),
                r=[projr, idx_tl.r], w=[dst_r] + list(extra_w))

        for layer in range(n_layers):
            even = layer % 2 == 0
            li = layer // 2
            N = 3072 if even else 1536
            NB = N // 128
            w_in = w_in_e[li] if even else w_in_o[li]
            rope_c0, rope_nh = (1536, 16) if even else (0, 20)
            qscale_chunks = (0, 3) if even else (0, 1)
            x_src = x_in if layer == 0 else x2_d[(layer - 1) % 2]
            x_src_r = None if layer == 0 else x2r[(layer - 1) % 2]
            x_dst = out if layer == n_layers - 1 else x2_d[layer % 2]
            x_dst_r = x2r[layer % 2]
            AF32.reset()
            ABF.reset()
            modst = [AF32.get([128, 3072], f"mods{s}") for s in range(2)]
            wa = [AF32.get([128, 8, 256], f"wa{i}") for i in range(2)]
            wb = ABF.get([128, 8, N], "wb")
            load_w_bf16(P, wb, w_in, N)
            k = 0
            for s in range(2):
                bcast_load(P, "sp", modst[s], b_ada[layer, s, :])
                for n in range(12):
                    wt = wa[k % 2]
                    pm = bank[k % 2]
                    k += 1
                    P.dma("sp", lambda e, wt=wt, s=s, n=n, layer=layer: e.dma_start(
                        out=wt.ap, in_=w_ada[layer, s, :, n * 256:(n + 1) * 256].rearrange("(kc p) n -> p kc n", p=128)),
                        w=[wt.r])

                    def mm(e, wt=wt, pm=pm):
                        for kc in range(8):
                            ins = e.matmul(pm.ap[:, 0:256], lhsT=cond_rep.ap[:, kc, :], rhs=wt.ap[:, kc, :],
                                           start=(kc == 0), stop=(kc == 7))
                        return ins
                    P.op("pe", mm, r=[cond_rep.r, wt.r], w=[pm.r])
                    P.op("dve", lambda e, pm=pm, s=s, n=n, modst=modst: e.tensor_tensor(
                        out=modst[s].ap[:, n * 256:(n + 1) * 256], in0=pm.ap[:, 0:256], in1=modst[s].ap[:, n * 256:(n + 1) * 256],
                        op=ALU.add), r=[pm.r, modst[s].r], w=[modst[s].r])
                P.dma("sp", lambda e, s=s, modst=modst: e.dma_start(out=mods_d[s:s + 1, :], in_=modst[s].ap[0:1, :]),
                      r=[modst[s].r], w=[modsr])
            shift_b = view(modst[0], modst[0].ap[:, 0:1024])
            scale_b = view(modst[0], modst[0].ap[:, 1024:2048])
            xt = [AF32.get([128, 1024], f"xt{i}") for i in range(2)]
            tmpA = AF32.get([128, 1024], "tmpA")
            hbA = ABF.get([128, 1024], "hbA")
            hTA = ABF.get([128, 8, 128], "hTA")
            stage = AF32.get([128, N], "stage")
            ob = [ABF.get([128, N], f"ob{i}") for i in range(2)]
            cs = [AF32.get([128, 2, 32], f"cs{i}") for i in range(2)]
            ra = AF32.get([128, rope_nh, 32], "ra")
            rb = AF32.get([128, rope_nh, 32], "rb")
            rc = AF32.get([128, rope_nh, 32], "rc")
            rd = AF32.get([128, rope_nh, 32], "rd")
            nch = N // 512
            for t in range(NT):
                x_t = xt[t % 2]
                o_t = ob[t % 2]
                c_t = cs[t % 2]
                P.dma("sp", lambda e, x_t=x_t, t=t, x_src=x_src: e.dma_start(out=x_t.ap, in_=x_src[t * 128:(t + 1) * 128, :]),
                      r=([x_src_r[t]] if x_src_r else []), w=[x_t.r])
                P.dma("sp", lambda e, c_t=c_t, t=t: e.dma_start(out=c_t.ap[:, 0, :], in_=cos[t * 128:(t + 1) * 128, :]), w=[c_t.r])
                P.dma("sp", lambda e, c_t=c_t, t=t: e.dma_start(out=c_t.ap[:, 1, :], in_=sin[t * 128:(t + 1) * 128, :]), w=[c_t.r])
                modulate_transpose(P, x_t, scale_b, shift_b, tmpA, hbA, hTA, psT, ident)
                for n in range(nch):
                    pm = bank[n % 2]

                    def mm(e, pm=pm, n=n, hTA=hTA, wb=wb):
                        for kc in range(8):
                            ins = e.matmul(pm.ap, lhsT=hTA.ap[:, kc, :], rhs=wb.ap[:, kc, n * 512:(n + 1) * 512],
                                           start=(kc == 0), stop=(kc == 7))
                        return ins
                    P.op("pe", mm, r=[hTA.r, wb.r], w=[pm.r])
                    sc = 0.125 if n in qscale_chunks else 1.0
                    P.op("act", lambda e, pm=pm, n=n, sc=sc, stage=stage: e.activation(
                        out=stage.ap[:, n * 512:(n + 1) * 512], in_=pm.ap, func=AF.Copy, scale=sc),
                        r=[pm.r], w=[stage.r])
                c1 = rope_c0 + rope_nh * 64
                V = stage.ap[:, rope_c0:c1].rearrange("p (h two d) -> p h two d", two=2, d=32)
                O = o_t.ap[:, rope_c0:c1].rearrange("p (h two d) -> p h two d", two=2, d=32)
                cosb = c_t.ap[:, 0:1, :].to_broadcast([128, rope_nh, 32])
                sinb = c_t.ap[:, 1:2, :].to_broadcast([128, rope_nh, 32])
                P.op("dve", lambda e, V=V, cosb=cosb, ra=ra: e.tensor_tensor(out=ra.ap, in0=V[:, :, 0, :], in1=cosb, op=ALU.mult),
                     r=[stage.r, c_t.r], w=[ra.r])
                P.op("dve", lambda e, V=V, sinb=sinb, rb=rb: e.tensor_tensor(out=rb.ap, in0=V[:, :, 1, :], in1=sinb, op=ALU.mult),
                     r=[stage.r, c_t.r], w=[rb.r])
                P.op("pool", lambda e, V=V, sinb=sinb, rc=rc: e.tensor_tensor(out=rc.ap, in0=V[:, :, 0, :], in1=sinb, op=ALU.mult),
                     r=[stage.r, c_t.r], w=[rc.r])
                P.op("pool", lambda e, V=V, cosb=cosb, rd=rd: e.tensor_tensor(out=rd.ap, in0=V[:, :, 1, :], in1=cosb, op=ALU.mult),
                     r=[stage.r, c_t.r], w=[rd.r])
                P.op("dve", lambda e, O=O, ra=ra, rb=rb: e.tensor_tensor(out=O[:, :, 0, :], in0=ra.ap, in1=rb.ap, op=ALU.subtract),
                     r=[ra.r, rb.r], w=[o_t.r])
                P.op("dve", lambda e, O=O, rc=rc, rd=rd: e.tensor_tensor(out=O[:, :, 1, :], in0=rc.ap, in1=rd.ap, op=ALU.add),
                     r=[rc.r, rd.r], w=[o_t.r])
                if rope_c0 > 0:
                    P.op("act", lambda e, o_t=o_t, stage=stage, rope_c0=rope_c0: e.copy(out=o_t.ap[:, 0:rope_c0], in_=stage.ap[:, 0:rope_c0]),
                         r=[stage.r], w=[o_t.r])
                if c1 < N:
                    P.op("act", lambda e, o_t=o_t, c1=c1, stage=stage, N=N: e.copy(out=o_t.ap[:, c1:N], in_=stage.ap[:, c1:N]),
                         r=[stage.r], w=[o_t.r])
                P.dma("sp", lambda e, o_t=o_t, t=t, NB=NB: e.dma_start(
                    out=proj_v[t * 128:(t + 1) * 128, 0:NB, :], in_=o_t.ap.rearrange("p (j c) -> p j c", c=128)),
                    r=[o_t.r], w=[projr])
            barrier(P)
            AF32.reset()
            ABF.reset()
            abf_mark = ABF.off
            gk = [ABF.get([128, 128], f"gk{i}") for i in range(3)]
            abf_mark2 = ABF.off

            def qload(dst, blk, NB=NB):
                for g8 in range(0, NT, 8):
                    n8 = min(8, NT - g8)
                    qst = gk_q
                    for u in range(n8):
                        i = g8 + u
                        P.dma("sp", lambda e, i=i, u=u, blk=blk: e.dma_start(
                            out=qst.ap[:, u, :], in_=proj_loc[blk * T + i * 128: blk * T + (i + 1) * 128, :]),
                            r=[projr], w=[qst.r])

                    def tr(e, n8=n8):
                        for u in range(n8):
                            ins = e.transpose(out=psT.ap[:, u, :], in_=qst.ap[:, u, :], identity=ident.ap)
                        return ins
                    P.op("pe", tr, r=[qst.r, ident.r], w=[psT.r])
                    P.op("act", lambda e, g8=g8, n8=n8, dst=dst: e.copy(
                        out=dst.ap[:, g8 * 128:(g8 + n8) * 128].rearrange("p (u c) -> p u c", c=128), in_=psT.ap[:, 0:n8, :]),
                        r=[psT.r], w=[dst.r])

            def kload(dst, idx_tl, col0, ntile, dup_half=None):
                for g8 in range(0, ntile, 8):
                    n8 = min(8, ntile - g8)
                    for u in range(n8):
                        p_ = g8 + u
                        gather(gk_q.ap[:, u, :], gk_q.r, idx_tl, col0 + p_)
                    if dup_half is not None:
                        src_h = dup_half
                        P.op("dve", lambda e, n8=n8, src_h=src_h: e.tensor_copy(
                            out=gk_q.ap[:, 0:n8, (1 - src_h) * 64:(2 - src_h) * 64], in_=gk_q.ap[:, 0:n8, src_h * 64:(src_h + 1) * 64]),
                            r=[gk_q.r], w=[gk_q.r])

                    def tr(e, n8=n8):
                        for u in range(n8):
                            ins = e.transpose(out=psT.ap[:, u, :], in_=gk_q.ap[:, u, :], identity=ident.ap)
                        return ins
                    P.op("pe", tr, r=[gk_q.r, ident.r], w=[psT.r])
                    P.op("act", lambda e, g8=g8, n8=n8, dst=dst: e.copy(
                        out=dst.ap[:, g8 * 128:(g8 + n8) * 128].rearrange("p (u c) -> p u c", c=128), in_=psT.ap[:, 0:n8, :]),
                        r=[psT.r], w=[dst.r])

            gk_q = ABF.get([128, 8, 128], "gk_q")
            if even:
                lv = AF32.get([128, 4, 64], "lv")
                lsm = AF32.get([128, 8], "lsm")
                junk64 = AF32.get([128, 64], "junk64")
                subg_b = AF32.get([128, 128], "subg_b")
                lamc_b = AF32.get([128, 2], "lamc_b")
                P.dma("sp", lambda e, li=li, lv=lv: e.dma_start(out=lv.ap, in_=lamv[li].rearrange("a d -> (a d)").partition_broadcast(128).rearrange("p (a d) -> p a d", d=64)), w=[lv.r])
                bcast_load(P, "sp", subg_b, subg[li, :])
                bcast_load(P, "sp", lamc_b, lamc[li, :])
                for k_ in range(2):
                    P.op("dve", lambda e, k_=k_, lv=lv, lsm=lsm, junk64=junk64: e.scalar_tensor_tensor(
                        out=junk64.ap, in0=lv.ap[:, 2 * k_, :], scalar=1.0, in1=lv.ap[:, 2 * k_ + 1, :],
                        op0=ALU.mult, op1=ALU.mult, accum_out=lsm.ap[:, k_:k_ + 1]), r=[lv.r], w=[junk64.r, lsm.r])
                P.op("act", lambda e, lsm=lsm: e.activation(out=lsm.ap[:, 2:4], in_=lsm.ap[:, 0:2], func=AF.Exp), r=[lsm.r], w=[lsm.r])
                P.op("dve", lambda e, lsm=lsm, lamc_b=lamc_b: e.tensor_scalar(
                    out=lsm.ap[:, 4:5], in0=lsm.ap[:, 2:3], scalar1=lsm.ap[:, 3:4], scalar2=lamc_b.ap[:, 0:1],
                    op0=ALU.subtract, op1=ALU.add), r=[lsm.r, lamc_b.r], w=[lsm.r])
                P.op("dve", lambda e, lsm=lsm: e.tensor_scalar(out=lsm.ap[:, 5:6], in0=lsm.ap[:, 4:5], scalar1=-1.0, scalar2=None,
                                                               op0=ALU.mult), r=[lsm.r], w=[lsm.r])
                P.op("dve", lambda e, subg_b=subg_b, lamc_b=lamc_b: e.tensor_scalar(
                    out=subg_b.ap, in0=subg_b.ap, scalar1=lamc_b.ap[:, 1:2], scalar2=None, op0=ALU.mult),
                    r=[subg_b.r, lamc_b.r], w=[subg_b.r])
                af_mark = AF32.off
                n_kT = ABF.get([128, NKA * 128], "n_kT")
                n_qT = ABF.get([128, T], "n_qT")
                n_v = ABF.get([128, NKA, 2, 65], "n_v")
                n_vs = ABF.get([128, 128], "n_vs")
                n_bias = ABF.get([128, 25, 128], "n_bias")
                n_PT = [ABF.get([128, 5, 128], f"n_PT{i}") for i in range(2)]
                rz1 = [AF32.get([128, 1], f"rz1_{i}") for i in range(2)]
                n_st = ABF.get([128, NT, 128], "n_st")
                for m in range(4):
                    qload(n_qT, m)
                    kload(n_kT, ina, m * NKA, NKA)
                    P.op("pool", lambda e, n_v=n_v: e.memset(n_v.ap, 1.0), w=[n_v.r])
                    for p_ in range(NKA):
                        gather(n_vs.ap, n_vs.r, ina, (4 + m) * NKA + p_)
                        P.op("dve", lambda e, p_=p_, n_v=n_v, n_vs=n_vs: e.tensor_copy(
                            out=n_v.ap[:, p_, :, 0:64], in_=n_vs.ap.rearrange("p (e d) -> p e d", d=64)),
                            r=[n_vs.r], w=[n_v.r])
                    for e_ in range(2):
                        h = 2 * m + e_
                        P.dma("sp", lambda e, h=h, li=li, n_bias=n_bias: e.dma_start(
                            out=n_bias.ap, in_=bias_na[li, h].rearrange("s p q -> p s q")), w=[n_bias.r])
                        for i in range(NT):
                            slot = {0: 0, 1: 1, NT - 2: 3, NT - 1: 4}.get(i, 2)
                            S = psA[i % 2]
                            Sr = [bank[(i % 2) * 2].r, bank[(i % 2) * 2 + 1].r]
                            pt = n_PT[i % 2]

                            def qk(e, S=S, i=i, slot=slot, e_=e_, n_kT=n_kT, n_qT=n_qT, n_bias=n_bias):
                                for j in range(5):
                                    e.matmul(S.ap[:, j * 128:(j + 1) * 128],
                                             lhsT=n_kT.ap[e_ * 64:(e_ + 1) * 64, (i + j) * 128:(i + j + 1) * 128],
                                             rhs=n_qT.ap[e_ * 64:(e_ + 1) * 64, i * 128:(i + 1) * 128], start=True, stop=False)
                                    ins = e.matmul(S.ap[:, j * 128:(j + 1) * 128], lhsT=ident.ap, rhs=n_bias.ap[:, slot * 5 + j, :],
                                                   start=False, stop=True)
                                return ins
                            P.op("pe", qk, r=[n_kT.r, n_qT.r, n_bias.r, ident.r], w=Sr)
                            P.op("act", lambda e, S=S, pt=pt: e.activation(
                                out=pt.ap, in_=S.ap[:, 0:640].rearrange("p (j q) -> p j q", q=128), func=AF.Exp),
                                r=Sr, w=[pt.r])
                            O = psO[i % 2]

                            def pvm(e, O=O, pt=pt, i=i, e_=e_, n_v=n_v):
                                for j in range(5):
                                    ins = e.matmul(O.ap[:, 0:65], lhsT=pt.ap[:, j, :], rhs=n_v.ap[:, i + j, e_, :],
                                                   start=(j == 0), stop=(j == 4))
                                return ins
                            P.op("pe", pvm, r=[pt.r, n_v.r], w=[O.r])
                            rz = rz1[i % 2]
                            P.op("dve", lambda e, O=O, rz=rz: e.reciprocal(out=rz.ap, in_=O.ap[:, 64:65]), r=[O.r], w=[rz.r])
                            P.op("dve", lambda e, O=O, rz=rz, i=i, e_=e_, n_st=n_st: e.tensor_scalar(
                                out=n_st.ap[:, i, e_ * 64:(e_ + 1) * 64], in0=O.ap[:, 0:64], scalar1=rz.ap[:, 0:1], scalar2=None,
                                op0=ALU.mult), r=[O.r, rz.r], w=[n_st.r])
                    P.dma("sp", lambda e, m=m, n_st=n_st: e.dma_start(out=attn_v[:, :, m * 128:(m + 1) * 128], in_=n_st.ap),
                          r=[n_st.r], w=[attnr])
                barrier(P)
                ABF.off = abf_mark2 + 8 * 128
                AF32.off = af_mark
                d_kT = ABF.get([128, 8192], "d_kT")
                d_qT = ABF.get([128, T], "d_qT")
                d_v = ABF.get([128, 64, 129], "d_v")
                d_PT = [ABF.get([128, 256], f"d_PT{i}") for i in range(2)]
                Oe = AF32.get([128, 2, 2, 129], "Oe")
                rzd = AF32.get([128, 8], "rzd")
                o1 = AF32.get([128, 128], "o1")
                o2 = AF32.get([128, 128], "o2")
                junk = AF32.get([128, 128], "junkd")
                d_st = ABF.get([128, NT, 128], "d_st")
                for h in range(4):
                    qload(d_qT, 12 + h)
                    kload(d_kT, ind, h * NKD, NKD)
                    P.op("pool", lambda e, d_v=d_v: e.memset(d_v.ap, 1.0), w=[d_v.r])
                    for kt in range(NKD):
                        gather(d_v.ap[:, kt, 0:128], d_v.r, ind, (4 + h) * NKD + kt)
                    for qb in range(NT // 2):
                        for m_ in range(2):
                            for kt in range(64):
                                S = bank[kt % 4]
                                pt = d_PT[kt % 2]
                                P.op("pe", lambda e, S=S, m_=m_, kt=kt, qb=qb, d_kT=d_kT, d_qT=d_qT: e.matmul(
                                    S.ap[:, 0:256], lhsT=d_kT.ap[m_ * 64:(m_ + 1) * 64, kt * 128:(kt + 1) * 128],
                                    rhs=d_qT.ap[m_ * 64:(m_ + 1) * 64, qb * 256:(qb + 1) * 256], start=True, stop=True),
                                    r=[d_kT.r, d_qT.r], w=[S.r])
                                P.op("act", lambda e, S=S, pt=pt: e.activation(out=pt.ap, in_=S.ap[:, 0:256], func=AF.Exp),
                                     r=[S.r], w=[pt.r])

                                def pvm(e, pt=pt, kt=kt, d_v=d_v):
                                    for qs in range(2):
                                        ins = e.matmul(psO[qs].ap[:, 0:129], lhsT=pt.ap[:, qs * 128:(qs + 1) * 128],
                                                       rhs=d_v.ap[:, kt, :], start=(kt == 0), stop=(kt == 63))
                                    return ins
                                P.op("pe", pvm, r=[pt.r, d_v.r], w=[psO[0].r, psO[1].r])
                            for qs in range(2):
                                P.op("act", lambda e, m_=m_, qs=qs, Oe=Oe: e.copy(out=Oe.ap[:, m_, qs, :], in_=psO[qs].ap[:, 0:129]),
                                     r=[psO[qs].r], w=[Oe.r])
                        for qs in range(2):
                            i = qb * 2 + qs
                            P.op("dve", lambda e, qs=qs, Oe=Oe, rzd=rzd: e.reciprocal(out=rzd.ap[:, 0:2], in_=Oe.ap[:, :, qs, 128]),
                                 r=[Oe.r], w=[rzd.r])
                            P.op("dve", lambda e, rzd=rzd, lsm=lsm: e.tensor_tensor(out=rzd.ap[:, 2:3], in0=rzd.ap[:, 1:2], in1=lsm.ap[:, 5:6],
                                                                                    op=ALU.mult), r=[rzd.r, lsm.r], w=[rzd.r])
                            P.op("dve", lambda e, qs=qs, Oe=Oe, rzd=rzd, o1=o1: e.tensor_scalar(
                                out=o1.ap, in0=Oe.ap[:, 0, qs, 0:128], scalar1=rzd.ap[:, 0:1], scalar2=None, op0=ALU.mult),
                                r=[Oe.r, rzd.r], w=[o1.r])
                            P.op("dve", lambda e, qs=qs, Oe=Oe, rzd=rzd, o1=o1, o2=o2: e.scalar_tensor_tensor(
                                out=o2.ap, in0=Oe.ap[:, 1, qs, 0:128], scalar=rzd.ap[:, 2:3], in1=o1.ap,
                                op0=ALU.mult, op1=ALU.add), r=[Oe.r, rzd.r, o1.r], w=[o2.r])
                            P.op("dve", lambda e, o2=o2, junk=junk, rzd=rzd: e.scalar_tensor_tensor(
                                out=junk.ap, in0=o2.ap, scalar=1.0, in1=o2.ap, op0=ALU.mult, op1=ALU.mult,
                                accum_out=rzd.ap[:, 3:4]), r=[o2.r], w=[junk.r, rzd.r])
                            P.op("act", lambda e, rzd=rzd: e.activation(out=rzd.ap[:, 4:5], in_=rzd.ap[:, 3:4], func=AF.Sqrt,
                                                                        bias=eps_t.ap[:, 0:1], scale=1.0 / 128.0),
                                 r=[rzd.r, eps_t.r], w=[rzd.r])
                            P.op("dve", lambda e, rzd=rzd: e.reciprocal(out=rzd.ap[:, 5:6], in_=rzd.ap[:, 4:5]), r=[rzd.r], w=[rzd.r])
                            P.op("dve", lambda e, i=i, o2=o2, rzd=rzd, subg_b=subg_b, d_st=d_st: e.scalar_tensor_tensor(
                                out=d_st.ap[:, i, :], in0=o2.ap, scalar=rzd.ap[:, 5:6],
                                in1=subg_b.ap, op0=ALU.mult, op1=ALU.mult), r=[o2.r, rzd.r, subg_b.r], w=[d_st.r])
                    P.dma("sp", lambda e, h=h, d_st=d_st: e.dma_start(out=attn_v[:, :, 512 + h * 128:512 + (h + 1) * 128], in_=d_st.ap),
                          r=[d_st.r], w=[attnr])
                barrier(P)
            else:
                esink = AF32.get([128, 16], "esink")
                bcast_load(P, "sp", esink, sink[li, :])
                P.op("act", lambda e, esink=esink: e.activation(out=esink.ap, in_=esink.ap, func=AF.Exp), r=[esink.r], w=[esink.r])
                c_kT = ABF.get([128, NKC * 128], "c_kT")
                c_qT = ABF.get([128, T], "c_qT")
                c_v = ABF.get([128, NKC, 65], "c_v")
                c_vs = ABF.get([128, 128], "c_vs")
                c_bias = ABF.get([128, 9, 128], "c_bias")
                c_PT = [ABF.get([128, 3, 128], f"c_PT{i}") for i in range(2)]
                den = [AF32.get([128, 2], f"den{i}") for i in range(2)]
                c_st = ABF.get([128, NT, 128], "c_st")
                P.dma("sp", lambda e, c_bias=c_bias: e.dma_start(out=c_bias.ap, in_=bias_swa.rearrange("s p q -> p s q")), w=[c_bias.r])
                for hk in range(4):
                    kload(c_kT, inc, (hk // 2) * NKC, NKC, dup_half=hk % 2)
                    P.op("pool", lambda e, c_v=c_v: e.memset(c_v.ap, 1.0), w=[c_v.r])
                    for p_ in range(NKC):
                        gather(c_vs.ap, c_vs.r, inc, (2 + hk // 2) * NKC + p_)
                        P.op("dve", lambda e, p_=p_, hk=hk, c_v=c_v, c_vs=c_vs: e.tensor_copy(
                            out=c_v.ap[:, p_, 0:64], in_=c_vs.ap[:, (hk % 2) * 64:(hk % 2 + 1) * 64]),
                            r=[c_vs.r], w=[c_v.r])
                    for gp in range(2):
                        qload(c_qT, hk * 2 + gp)
                        for e_ in range(2):
                            hq = hk * 4 + gp * 2 + e_
                            for i in range(NT):
                                slot = 0 if i == 0 else (2 if i == NT - 1 else 1)
                                S = bank[i % 4]
                                pt = c_PT[i % 2]

                                def qk(e, S=S, i=i, slot=slot, e_=e_, c_kT=c_kT, c_qT=c_qT, c_bias=c_bias):
                                    for j in range(3):
                                        e.matmul(S.ap[:, j * 128:(j + 1) * 128],
                                                 lhsT=c_kT.ap[e_ * 64:(e_ + 1) * 64, (i + j) * 128:(i + j + 1) * 128],
                                                 rhs=c_qT.ap[e_ * 64:(e_ + 1) * 64, i * 128:(i + 1) * 128], start=True, stop=False)
                                        ins = e.matmul(S.ap[:, j * 128:(j + 1) * 128], lhsT=ident.ap, rhs=c_bias.ap[:, slot * 3 + j, :],
                                                       start=False, stop=True)
                                    return ins
                                P.op("pe", qk, r=[c_kT.r, c_qT.r, c_bias.r, ident.r], w=[S.r])
                                P.op("act", lambda e, S=S, pt=pt: e.activation(
                                    out=pt.ap, in_=S.ap[:, 0:384].rearrange("p (j q) -> p j q", q=128), func=AF.Exp),
                                    r=[S.r], w=[pt.r])
                                O = psO[i % 2]

                                def pvm(e, O=O, pt=pt, i=i, c_v=c_v):
                                    for j in range(3):
                                        ins = e.matmul(O.ap[:, 0:65], lhsT=pt.ap[:, j, :], rhs=c_v.ap[:, i + j, :],
                                                       start=(j == 0), stop=(j == 2))
                                    return ins
                                P.op("pe", pvm, r=[pt.r, c_v.r], w=[O.r])
                                dn = den[i % 2]
                                P.op("dve", lambda e, O=O, dn=dn, hq=hq, esink=esink: e.tensor_tensor(
                                    out=dn.ap[:, 0:1], in0=O.ap[:, 64:65], in1=esink.ap[:, hq:hq + 1], op=ALU.add),
                                    r=[O.r, esink.r], w=[dn.r])
                                P.op("dve", lambda e, dn=dn: e.reciprocal(out=dn.ap[:, 1:2], in_=dn.ap[:, 0:1]), r=[dn.r], w=[dn.r])
                                P.op("dve", lambda e, O=O, dn=dn, i=i, e_=e_, c_st=c_st: e.tensor_scalar(
                                    out=c_st.ap[:, i, e_ * 64:(e_ + 1) * 64], in0=O.ap[:, 0:64], scalar1=dn.ap[:, 1:2], scalar2=None,
                                    op0=ALU.mult), r=[O.r, dn.r], w=[c_st.r])
                        P.dma("sp", lambda e, hk=hk, gp=gp, c_st=c_st: e.dma_start(
                            out=attn_v[:, :, (hk * 2 + gp) * 128:(hk * 2 + gp + 1) * 128], in_=c_st.ap), r=[c_st.r], w=[attnr])
                barrier(P)
            ABF.off = abf_mark
            AF32.reset()
            if dbg:
                P.dma("sp", lambda e, layer=layer: e.dma_start(out=dbg_o["attn"][layer], in_=attn_d), r=[attnr], w=[P.out_r])
            w_out = w_out_e[li] if even else w_out_o[li]
            wo = ABF.get([128, 8, 1024], "wo")
            aT = ABF.get([128, 8, 128], "aT")
            load_w_bf16(P, wo, w_out, 1024)
            gate1 = AF32.get([128, 1024], "gate1")
            g0 = AF32.get([128, 1024], "g0")
            b0 = AF32.get([128, 1024], "b0")
            P.dma("sp", lambda e, gate1=gate1: e.dma_start(out=gate1.ap, in_=mods_d[0, 2048:3072].partition_broadcast(128)),
                  r=[modsr], w=[gate1.r])
            bcast_load(P, "sp", g0, ln_g[layer, 0, :])
            bcast_load(P, "sp", b0, ln_b[layer, 0, :])
            xk = [AF32.get([128, 1024], f"xk{i}") for i in range(2)]
            atk = [ABF.get([128, 1024], f"atk{i}") for i in range(2)]
            ta = AF32.get([128, 1024], "ta")
            tb = AF32.get([128, 1024], "tb")
            st = AF32.get([128, 16], "st")
            for i in range(NT):
                x_t = xk[i % 2]
                P.dma("sp", lambda e, x_t=x_t, i=i, x_src=x_src: e.dma_start(out=x_t.ap, in_=x_src[i * 128:(i + 1) * 128, :]),
                      r=([x_src_r[i]] if x_src_r else []), w=[x_t.r])

                at_t = atk[i % 2]
                P.dma("sp", lambda e, at_t=at_t, i=i: e.dma_start(out=at_t.ap, in_=attn_d[i * 128:(i + 1) * 128, :]),
                      r=[attnr], w=[at_t.r])

                def tr(e, at_t=at_t):
                    for k_ in range(8):
                        ins = e.transpose(out=psT.ap[:, k_, :], in_=at_t.ap[:, k_ * 128:(k_ + 1) * 128], identity=ident.ap)
                    return ins
                P.op("pe", tr, r=[at_t.r, ident.r], w=[psT.r])
                P.op("act", lambda e, aT=aT: e.copy(out=aT.ap, in_=psT.ap), r=[psT.r], w=[aT.r])
                Y = psA[i % 2]
                Yr = [bank[(i % 2) * 2].r, bank[(i % 2) * 2 + 1].r]

                def mm(e, Y=Y, aT=aT, wo=wo):
                    for n in range(2):
                        for kc in range(8):
                            ins = e.matmul(Y.ap[:, n * 512:(n + 1) * 512], lhsT=aT.ap[:, kc, :], rhs=wo.ap[:, kc, n * 512:(n + 1) * 512],
                                           start=(kc == 0), stop=(kc == 7))
                    return ins
                P.op("pe", mm, r=[aT.r, wo.r], w=Yr)
                P.op("dve", lambda e, Y=Y, ta=ta, gate1=gate1: e.tensor_tensor(out=ta.ap, in0=Y.ap, in1=gate1.ap, op=ALU.mult),
                     r=Yr + [gate1.r], w=[ta.r])
                P.op("dve", lambda e, x_t=x_t, ta=ta, tb=tb: e.scalar_tensor_tensor(out=tb.ap, in0=x_t.ap, scalar=ALPHA, in1=ta.ap,
                                                                               op0=ALU.mult, op1=ALU.add), r=[x_t.r, ta.r], w=[tb.r])
                ln_only(P, tb, x_t, g0, b0, ta, st, eps_t)
                P.dma("sp", lambda e, x_t=x_t, i=i: e.dma_start(out=x1_d[i * 128:(i + 1) * 128, :], in_=x_t.ap),
                      r=[x_t.r], w=[x1r[i]])
                if dbg:
                    P.dma("sp", lambda e, x_t=x_t, i=i, layer=layer: e.dma_start(out=dbg_o["x1"][layer, i * 128:(i + 1) * 128, :], in_=x_t.ap),
                          r=[x_t.r], w=[P.out_r])
            barrier(P)
            ABF.reset()
            AF32.reset()
            emit_peer(P, nc, NT, layer, AF32, ABF, bank, psT, ident, eps_t, io16, eoff_t,
                      wq_all[layer], skT_all[layer], pu_all, pv_all, mods_d, modsr, ln_g, ln_b,
                      x1_d, x1r, x_dst, x_dst_r, dbg_o.get("x2"))
            barrier(P)
        P.emit()
    return nc


def emit_peer(P, nc, NT, layer, AF32, ABF, bank, psT, ident, eps_t, io16, eoff_t, wq, skT, pu, pv, mods_d, modsr,
              ln_g, ln_b, src, src_r, dst, dst_r, dbg_x2):
    wqb = ABF.get([128, 8, 2048], "wqb")
    skb = ABF.get([128, 16, 128], "skb")
    hb = ABF.get([128, 1024], "hb")
    hT = ABF.get([128, 8, 128], "hT")
    qTs = [ABF.get([128, 128], f"qTs{i}") for i in range(2)]
    load_w_bf16(P, wqb, wq, 2048)
    P.dma("pool", lambda e: e.dma_start(out=skb.ap, in_=skT.rearrange("hp e n -> e hp n")), w=[skb.r])
    scale2 = AF32.get([128, 1024], "scale2")
    shift2 = AF32.get([128, 1024], "shift2")
    gate2 = AF32.get([128, 1024], "gate2")
    g1 = AF32.get([128, 1024], "g1")
    b1 = AF32.get([128, 1024], "b1")
    for dst_t, c0 in ((shift2, 0), (scale2, 1024), (gate2, 2048)):
        P.dma("sp", lambda e, dst_t=dst_t, c0=c0: e.dma_start(out=dst_t.ap, in_=mods_d[1, c0:c0 + 1024].partition_broadcast(128)),
              r=[modsr], w=[dst_t.r])
    bcast_load(P, "sp", g1, ln_g[layer, 1, :])
    bcast_load(P, "sp", b1, ln_b[layer, 1, :])
    x_t = AF32.get([128, 1024], "px")
    h2 = AF32.get([128, 1024], "h2")
    tmp = AF32.get([128, 1024], "ptmp")
    acc = AF32.get([128, 1024], "acc")
    G = [AF32.get([128, 1024], f"G{i}") for i in range(3)]
    s_all = AF32.get([128, 16, 128], "s_all")
    cand = AF32.get([128, 8, 256], "cand")
    wk = AF32.get([128, 256], "wk")
    sv = AF32.get([128, 16, 16], "sv")
    si = AF32.get([128, 16, 16], "si")
    sif = AF32.get([128, 16, 16], "sif")
    fv = AF32.get([128, 8, 16], "fv")
    fpos = AF32.get([128, 8, 16], "fpos")
    ab_u = AF32.get([128, 2, 128], "ab_u")
    ab_f = AF32.get([128, 2, 128], "ab_f")
    oh = AF32.get([128, 8, 16, 16], "oh")
    iab = AF32.get([128, 2, 128], "iab")
    eidx_f = AF32.get([128, 128], "eidx_f")
    eidx = AF32.get([128, 128], "eidx")
    gts = AF32.get([128, 8, 16], "gts")
    zs = AF32.get([128, 16], "zs")
    actv = AF32.get([128, 128], "actv")
    wgt = AF32.get([128, 128], "wgt")
    st = AF32.get([128, 16], "pst")
    si_u = si.ap.bitcast(U32)
    fpos_u = fpos.ap.bitcast(U32)
    ab_uu = ab_u.ap.bitcast(U32)
    eidx_u = eidx.ap.bitcast(U32)
    sv4 = sv.ap.rearrange("p (h two) k -> p h two k", two=2)
    sif4 = sif.ap.rearrange("p (h two) k -> p h two k", two=2)
    kq = 0
    for i in range(NT):
        P.dma("sp", lambda e, i=i: e.dma_start(out=x_t.ap, in_=src[i * 128:(i + 1) * 128, :]), r=[src_r[i]], w=[x_t.r])
        modulate_transpose(P, x_t, scale2, shift2, tmp, hb, hT, psT, ident, h_f32=h2)
        for hp in range(16):
            qps = bank[kq % 4]
            sps = bank[(kq + 2) % 4]
            qs_ = qTs[kq % 2]
            kq += 1

            def qm(e, qps=qps, hp=hp):
                for kc in range(8):
                    ins = e.matmul(qps.ap[:, 0:128], lhsT=wqb.ap[:, kc, hp * 128:(hp + 1) * 128], rhs=hT.ap[:, kc, :],
                                   start=(kc == 0), stop=(kc == 7))
                return ins
            P.op("pe", qm, r=[wqb.r, hT.r], w=[qps.r])
            P.op("act", lambda e, qps=qps, qs_=qs_: e.copy(out=qs_.ap, in_=qps.ap[:, 0:128]), r=[qps.r], w=[qs_.r])
            P.op("pe", lambda e, sps=sps, qs_=qs_, hp=hp: e.matmul(sps.ap[:, 0:128], lhsT=qs_.ap, rhs=skb.ap[:, hp, :],
                                                                  start=True, stop=True), r=[qs_.r, skb.r], w=[sps.r])
            P.op("act", lambda e, sps=sps, hp=hp: e.copy(out=s_all.ap[:, hp, :], in_=sps.ap[:, 0:128]),
                 r=[sps.r], w=[s_all.r])
        for hp in range(16):
            P.op("dve", lambda e, hp=hp: e.max(out=sv.ap[:, hp, 0:8], in_=s_all.ap[:, hp, :]), r=[s_all.r], w=[sv.r])
            P.op("dve", lambda e, hp=hp: e.match_replace(out=wk.ap[:, 0:128], in_to_replace=sv.ap[:, hp, 0:8],
                                                         in_values=s_all.ap[:, hp, :], imm_value=-1e30),
                 r=[sv.r, s_all.r], w=[wk.r])
            P.op("dve", lambda e, hp=hp: e.max(out=sv.ap[:, hp, 8:16], in_=wk.ap[:, 0:128]), r=[wk.r], w=[sv.r])
            P.op("dve", lambda e, hp=hp: e.max_index(out=si_u[:, hp, 0:8], in_max=sv.ap[:, hp, 0:8],
                                                     in_values=s_all.ap[:, hp, :]), r=[sv.r, s_all.r], w=[si.r])
            P.op("dve", lambda e, hp=hp: e.max_index(out=si_u[:, hp, 8:16], in_max=sv.ap[:, hp, 8:16],
                                                     in_values=wk.ap[:, 0:128]), r=[sv.r, wk.r], w=[si.r])
        P.op("dve", lambda e: e.tensor_copy(out=sif.ap, in_=si_u), r=[si.r], w=[sif.r])
        P.op("dve", lambda e: e.tensor_tensor(
            out=cand.ap.rearrange("p h (a b) -> p h a b", b=16),
            in0=sv4[:, :, 0, :].unsqueeze(3).to_broadcast([128, 8, 16, 16]),
            in1=sv4[:, :, 1, :].unsqueeze(2).to_broadcast([128, 8, 16, 16]), op=ALU.add), r=[sv.r], w=[cand.r])
        for h in range(8):
            P.op("dve", lambda e, h=h: e.max(out=fv.ap[:, h, 0:8], in_=cand.ap[:, h, :]), r=[cand.r], w=[fv.r])
            P.op("dve", lambda e, h=h: e.match_replace(out=wk.ap, in_to_replace=fv.ap[:, h, 0:8],
                                                       in_values=cand.ap[:, h, :], imm_value=-1e30),
                 r=[fv.r, cand.r], w=[wk.r])
            P.op("dve", lambda e, h=h: e.max(out=fv.ap[:, h, 8:16], in_=wk.ap), r=[wk.r], w=[fv.r])
            P.op("dve", lambda e, h=h: e.max_index(out=fpos_u[:, h, 0:8], in_max=fv.ap[:, h, 0:8],
                                                   in_values=cand.ap[:, h, :]), r=[fv.r, cand.r], w=[fpos.r])
            P.op("dve", lambda e, h=h: e.max_index(out=fpos_u[:, h, 8:16], in_max=fv.ap[:, h, 8:16],
                                                   in_values=wk.ap), r=[fv.r, wk.r], w=[fpos.r])
        fpos_flat = fpos_u.rearrange("p h k -> p (h k)")
        P.op("dve", lambda e: e.tensor_single_scalar(out=ab_uu[:, 0, :], in_=fpos_flat, scalar=4,
                                                     op=ALU.logical_shift_right), r=[fpos.r], w=[ab_u.r])
        P.op("dve", lambda e: e.tensor_single_scalar(out=ab_uu[:, 1, :], in_=fpos_flat, scalar=15,
                                                     op=ALU.bitwise_and), r=[fpos.r], w=[ab_u.r])
        P.op("dve", lambda e: e.tensor_copy(out=ab_f.ap, in_=ab_uu), r=[ab_u.r], w=[ab_f.r])
        for p_ in range(2):
            abv = ab_f.ap[:, p_, :].rearrange("p (h k) -> p h k", k=16)
            P.op("dve", lambda e, abv=abv: e.tensor_tensor(
                out=oh.ap, in0=abv.unsqueeze(3).to_broadcast([128, 8, 16, 16]),
                in1=io16.ap.unsqueeze(1).unsqueeze(1).to_broadcast([128, 8, 16, 16]), op=ALU.is_equal),
                r=[ab_f.r, io16.r], w=[oh.r])
            P.op("dve", lambda e, p_=p_: e.tensor_tensor(
                out=oh.ap, in0=oh.ap, in1=sif4[:, :, p_, :].unsqueeze(2).to_broadcast([128, 8, 16, 16]), op=ALU.mult),
                r=[oh.r, sif.r], w=[oh.r])
            P.op("dve", lambda e, p_=p_: e.tensor_reduce(
                out=iab.ap[:, p_, :].rearrange("p (h k) -> p h k", k=16), in_=oh.ap, axis=AX.X, op=ALU.add),
                r=[oh.r], w=[iab.r])
        P.op("dve", lambda e: e.scalar_tensor_tensor(out=eidx_f.ap, in0=iab.ap[:, 0, :], scalar=128.0, in1=iab.ap[:, 1, :],
                                                     op0=ALU.mult, op1=ALU.add), r=[iab.r], w=[eidx_f.r])
        if eoff_t is not None:
            P.op("dve", lambda e: e.tensor_scalar(out=eidx_f.ap, in0=eidx_f.ap, scalar1=eoff_t.ap[:, layer:layer + 1], scalar2=None,
                                                  op0=ALU.add), r=[eidx_f.r, eoff_t.r], w=[eidx_f.r])
        P.op("dve", lambda e: e.tensor_copy(out=eidx_u, in_=eidx_f.ap), r=[eidx_f.r], w=[eidx.r])
        P.op("dve", lambda e: e.tensor_tensor(out=gts.ap, in0=fv.ap, in1=fv.ap[:, :, 0:1].to_broadcast([128, 8, 16]),
                                              op=ALU.subtract), r=[fv.r], w=[gts.r])
        P.op("act", lambda e: e.activation(out=gts.ap, in_=gts.ap, func=AF.Exp), r=[gts.r], w=[gts.r])
        P.op("dve", lambda e: e.tensor_reduce(out=zs.ap[:, 0:8], in_=gts.ap, axis=AX.X, op=ALU.add), r=[gts.r], w=[zs.r])
        P.op("dve", lambda e: e.reciprocal(out=zs.ap[:, 8:16], in_=zs.ap[:, 0:8]), r=[zs.r], w=[zs.r])
        P.op("dve", lambda e: e.tensor_tensor(out=gts.ap, in0=gts.ap, in1=zs.ap[:, 8:16].unsqueeze(2).to_broadcast([128, 8, 16]),
                                              op=ALU.mult), r=[gts.r, zs.r], w=[gts.r])
        for j in range(128):
            g_ = G[j % 3]
            P.dma("pool", lambda e, g_=g_, j=j: e.indirect_dma_start(
                out=g_.ap, out_offset=None, in_=pu,
                in_offset=bass.IndirectOffsetOnAxis(ap=eidx_u[:, j:j + 1], axis=0)), r=[eidx.r], w=[g_.r])
            P.op("dve", lambda e, g_=g_, j=j: e.scalar_tensor_tensor(
                out=tmp.ap, in0=g_.ap, scalar=1.0, in1=h2.ap, op0=ALU.mult, op1=ALU.mult,
                accum_out=actv.ap[:, j:j + 1]), r=[g_.r, h2.r], w=[tmp.r, actv.r])
        P.op("act", lambda e: e.activation(out=wgt.ap, in_=actv.ap, func=AF.Gelu), r=[actv.r], w=[wgt.r])
        P.op("dve", lambda e: e.tensor_tensor(out=wgt.ap, in0=wgt.ap, in1=gts.ap.rearrange("p h k -> p (h k)"),
                                              op=ALU.mult), r=[wgt.r, gts.r], w=[wgt.r])
        for j in range(128):
            g_ = G[(j + 2) % 3]
            P.dma("pool", lambda e, g_=g_, j=j: e.indirect_dma_start(
                out=g_.ap, out_offset=None, in_=pv,
                in_offset=bass.IndirectOffsetOnAxis(ap=eidx_u[:, j:j + 1], axis=0)), r=[eidx.r], w=[g_.r])
            if j == 0:
                P.op("dve", lambda e, g_=g_: e.tensor_scalar(out=acc.ap, in0=g_.ap, scalar1=wgt.ap[:, 0:1], scalar2=None,
                                                             op0=ALU.mult), r=[g_.r, wgt.r], w=[acc.r])
            else:
                P.op("dve", lambda e, g_=g_, j=j: e.scalar_tensor_tensor(
                    out=acc.ap, in0=g_.ap, scalar=wgt.ap[:, j:j + 1], in1=acc.ap, op0=ALU.mult, op1=ALU.add),
                    r=[g_.r, wgt.r, acc.r], w=[acc.r])
        ln_epilogue(P, acc, x_t, gate2, g1, b1, tmp, h2, st, eps_t)
        P.dma("sp", lambda e, i=i: e.dma_start(out=dst[i * 128:(i + 1) * 128, :], in_=x_t.ap), r=[x_t.r], w=[dst_r[i]])
        if dbg_x2 is not None:
            P.dma("sp", lambda e, i=i: e.dma_start(out=dbg_x2[layer, i * 128:(i + 1) * 128, :], in_=x_t.ap), r=[x_t.r], w=[P.out_r])
```
